# Optimizing a Trainium2 kernel written in Bass

```python
import math
import jax, jax.numpy as jnp
from jax import lax
import numpy as np

D_MODEL = 1024
BATCH = 8
SEQ = 4096
DEPTH = 1

DA_HEADS = 4
DA_QK_DIM = 64
DA_V_DIM = 2 * DA_QK_DIM
DA_WIDTH = DA_HEADS * DA_V_DIM
ML_HEADS = 4
ML_DIM = 128
ML_WIDTH = ML_HEADS * ML_DIM
MIX_WIDTH = DA_WIDTH + ML_WIDTH
ROPE_THETA = 500000.0
ROPE_DIM = DA_QK_DIM // 4
D_FF = 2816
CONV_K = 4
CHUNK = 64
Q_BLOCK = 128
EPS = 1e-6
IN_SIZES = (2 * DA_HEADS * DA_QK_DIM,
            2 * DA_HEADS * DA_QK_DIM,
            DA_WIDTH,
            2 * ML_WIDTH,
            ML_WIDTH,
            ML_WIDTH,
            ML_HEADS,
            ML_HEADS)
N_IN = sum(IN_SIZES)
IN_SPLITS = tuple(int(s) for s in np.cumsum(IN_SIZES)[:-1])

kernel_name = "hybrid_diffattn_mlstm_macaron_adaln"


def rmsnorm(t, g):
    tf = t.astype(jnp.float32)
    tf = tf * lax.rsqrt(jnp.mean(tf * tf, axis=-1, keepdims=True) + EPS)
    return tf.astype(t.dtype) * g


def swiglu(h, w12, w3):
    a, b = jnp.split(h @ w12, 2, axis=-1)
    return (jax.nn.silu(a) * b) @ w3


def causal_conv(u, w, b):
    s = u.shape[1]
    up = jnp.pad(u, ((0, 0), (CONV_K - 1, 0), (0, 0)))
    out = b
    for j in range(CONV_K):
        out = out + up[:, j:j + s] * w[j]
    return out


def partial_rope(t, cos, sin):
    tr, tp = t[..., :ROPE_DIM], t[..., ROPE_DIM:]
    x1, x2 = jnp.split(tr, 2, axis=-1)
    rot = jnp.concatenate([-x2, x1], axis=-1)
    return jnp.concatenate([tr * cos + rot * sin, tp], axis=-1)


def diff_attention(q, k, v, g_q, g_k, lam_vecs, g_out, lambda_init):
    bsz, s, _ = q.shape
    q = rmsnorm(q.reshape(bsz, s, DA_HEADS, 2, DA_QK_DIM), g_q)
    k = rmsnorm(k.reshape(bsz, s, DA_HEADS, 2, DA_QK_DIM), g_k)
    pos = jnp.arange(s, dtype=jnp.float32)
    inv_freq = ROPE_THETA ** (-jnp.arange(0, ROPE_DIM, 2, dtype=jnp.float32) / ROPE_DIM)
    ang = pos[:, None] * inv_freq[None, :]
    ang = jnp.concatenate([ang, ang], axis=-1)[:, None, None, :]
    cos, sin = jnp.cos(ang).astype(q.dtype), jnp.sin(ang).astype(q.dtype)
    q = partial_rope(q, cos, sin).transpose(0, 2, 3, 1, 4)
    k = partial_rope(k, cos, sin).transpose(0, 2, 3, 1, 4)
    v = v.reshape(bsz, s, DA_HEADS, DA_V_DIM).transpose(0, 2, 1, 3)
    lv = lam_vecs.astype(jnp.float32)
    lam = jnp.exp(jnp.sum(lv[0] * lv[1])) - jnp.exp(jnp.sum(lv[2] * lv[3])) + lambda_init
    scale = DA_QK_DIM ** -0.5
    kpos = jnp.arange(s)

    def block(i):
        qs = lax.dynamic_slice_in_dim(q, i * Q_BLOCK, Q_BLOCK, axis=3)
        sc = jnp.einsum('bhcqd,bhckd->bhcqk', qs, k).astype(jnp.float32) * scale
        qpos = i * Q_BLOCK + jnp.arange(Q_BLOCK)
        sc = jnp.where(kpos[None, :] <= qpos[:, None], sc, -jnp.inf)
        p = jax.nn.softmax(sc, axis=-1)
        a = (p[:, :, 0] - lam * p[:, :, 1]).astype(v.dtype)
        return jnp.einsum('bhqk,bhkd->bhqd', a, v)

    o = lax.map(block, jnp.arange(s // Q_BLOCK))
    o = o.transpose(1, 0, 3, 2, 4).reshape(bsz, s, DA_HEADS, DA_V_DIM)
    o = rmsnorm(o, g_out) * (1.0 - lambda_init)
    return o.reshape(bsz, s, DA_WIDTH)


def mlstm(q, k, v, o_pre, i_pre, f_pre, g_out):
    dtype = q.dtype
    bsz, s, _ = q.shape
    nc = s // CHUNK

    def to_chunks(t):
        return t.astype(jnp.float32).reshape(bsz, nc, CHUNK, ML_HEADS, ML_DIM).transpose(1, 0, 3, 2, 4)

    def gates_to_chunks(t):
        return t.astype(jnp.float32).reshape(bsz, nc, CHUNK, ML_HEADS).transpose(1, 0, 3, 2)

    qc = to_chunks(q) * (ML_DIM ** -0.5)
    kc, vc = to_chunks(k), to_chunks(v)
    ic = gates_to_chunks(i_pre)
    fc = jax.nn.log_sigmoid(gates_to_chunks(f_pre))
    tri = jnp.arange(CHUNK)[:, None] >= jnp.arange(CHUNK)[None, :]

    def step(carry, inp):
        C, n, m = carry
        qb, kb, vb, ib, fb = inp
        b = jnp.cumsum(fb, axis=-1)
        D = jnp.where(tri, b[..., :, None] - b[..., None, :] + ib[..., None, :], -jnp.inf)
        inter = b + m[..., None]
        m_t = jnp.maximum(inter, jnp.max(D, axis=-1))
        W = jnp.einsum('bhtd,bhsd->bhts', qb, kb) * jnp.exp(D - m_t[..., None])
        e_inter = jnp.exp(inter - m_t)
        num = e_inter[..., None] * jnp.einsum('bhtd,bhde->bhte', qb, C) + jnp.einsum('bhts,bhse->bhte', W, vb)
        den = e_inter * jnp.einsum('bhtd,bhd->bht', qb, n) + jnp.sum(W, axis=-1)
        h = num / jnp.maximum(jnp.abs(den), jnp.exp(-m_t))[..., None]
        g = b[..., -1]
        a = g[..., None] - b + ib
        m_new = jnp.maximum(g + m, jnp.max(a, axis=-1))
        decay = jnp.exp(g + m - m_new)
        w = jnp.exp(a - m_new[..., None])
        C_new = decay[..., None, None] * C + jnp.einsum('bhs,bhsd,bhse->bhde', w, kb, vb)
        n_new = decay[..., None] * n + jnp.einsum('bhs,bhsd->bhd', w, kb)
        return (C_new, n_new, m_new), h

    init = (jnp.zeros((bsz, ML_HEADS, ML_DIM, ML_DIM), jnp.float32),
            jnp.zeros((bsz, ML_HEADS, ML_DIM), jnp.float32),
            jnp.zeros((bsz, ML_HEADS), jnp.float32))
    _, h = lax.scan(step, init, (qc, kc, vc, ic, fc))
    h = h.transpose(1, 0, 3, 2, 4).reshape(bsz, s, ML_HEADS, ML_DIM).astype(dtype)
    h = rmsnorm(h, 1.0).reshape(bsz, s, ML_WIDTH) * g_out
    return jax.nn.sigmoid(o_pre) * h


def token_mix(h, w_in, conv_w, conv_b, b_igate, b_fgate, g_qnorm, g_knorm,
              lambda_qk, g_da_out, g_ml_out, w_out, lambda_init):
    u = h @ w_in
    da_q, da_k, da_v, ml_qk, ml_v, ml_o, ml_i, ml_f = jnp.split(u, IN_SPLITS, axis=-1)
    y_da = diff_attention(da_q, da_k, da_v, g_qnorm, g_knorm, lambda_qk, g_da_out, lambda_init)
    ml_qk = jax.nn.silu(causal_conv(ml_qk, conv_w, conv_b))
    ml_q, ml_k = jnp.split(ml_qk, 2, axis=-1)
    y_ml = mlstm(ml_q, ml_k, ml_v, ml_o, ml_i + b_igate, ml_f + b_fgate, g_ml_out)
    return jnp.concatenate([y_da, y_ml], axis=-1) @ w_out


def setup_inputs(seed: int = 0) -> dict:
    key = jax.random.key(seed)
    ks = jax.random.split(key, 24)
    f32 = jnp.float32
    nrm = lambda k, shape, s: jax.random.normal(k, shape, f32) * s
    L = DEPTH
    return {
        "x": jax.random.normal(ks[0], (BATCH, SEQ, D_MODEL), f32),
        "c": jax.random.normal(ks[1], (BATCH, D_MODEL), f32),
        "w_ada": nrm(ks[2], (L, D_MODEL, 9 * D_MODEL), 0.1 * D_MODEL ** -0.5),
        "b_ada": nrm(ks[3], (L, 9 * D_MODEL), 0.02),
        "g_norm": 1.0 + nrm(ks[4], (L, 3, D_MODEL), 0.02),
        "ffn1_w12": nrm(ks[5], (L, D_MODEL, 2 * D_FF), D_MODEL ** -0.5),
        "ffn1_w3": nrm(ks[6], (L, D_FF, D_MODEL), D_FF ** -0.5),
        "w_in": nrm(ks[7], (L, D_MODEL, N_IN), D_MODEL ** -0.5),
        "conv_w": nrm(ks[8], (L, CONV_K, 2 * ML_WIDTH), CONV_K ** -0.5),
        "conv_b": nrm(ks[9], (L, 2 * ML_WIDTH), 0.02),
        "b_igate": nrm(ks[10], (L, ML_HEADS), 0.1),
        "b_fgate": 3.0 + 3.0 * jax.random.uniform(ks[11], (L, ML_HEADS), f32),
        "g_qnorm": 1.0 + nrm(ks[12], (L, DA_QK_DIM), 0.02),
        "g_knorm": 1.0 + nrm(ks[13], (L, DA_QK_DIM), 0.02),
        "lambda_qk": nrm(ks[14], (L, 4, DA_QK_DIM), 0.1),
        "g_da_out": 1.0 + nrm(ks[15], (L, DA_V_DIM), 0.02),
        "g_ml_out": 1.0 + nrm(ks[16], (L, ML_WIDTH), 0.02),
        "w_out": nrm(ks[17], (L, MIX_WIDTH, D_MODEL), MIX_WIDTH ** -0.5),
        "ffn2_w12": nrm(ks[18], (L, D_MODEL, 2 * D_FF), D_MODEL ** -0.5),
        "ffn2_w3": nrm(ks[19], (L, D_FF, D_MODEL), D_FF ** -0.5),
    }


def reference(x, c, w_ada, b_ada, g_norm, ffn1_w12, ffn1_w3, w_in, conv_w, conv_b,
              b_igate, b_fgate, g_qnorm, g_knorm, lambda_qk, g_da_out, g_ml_out,
              w_out, ffn2_w12, ffn2_w3):
    bsz = x.shape[0]
    cs = jax.nn.silu(c)
    for l in range(DEPTH):
        lambda_init = 0.8 - 0.6 * math.exp(-0.3 * l)
        mod = (cs @ w_ada[l] + b_ada[l]).reshape(bsz, 3, 3, D_MODEL)
        shift, scale, gate = mod[:, :, 0], mod[:, :, 1], mod[:, :, 2]

        h = rmsnorm(x, g_norm[l, 0]) * (1.0 + scale[:, 0, None]) + shift[:, 0, None]
        x = x + 0.5 * (1.0 + gate[:, 0, None]) * swiglu(h, ffn1_w12[l], ffn1_w3[l])

        h = rmsnorm(x, g_norm[l, 1]) * (1.0 + scale[:, 1, None]) + shift[:, 1, None]
        x = x + (1.0 + gate[:, 1, None]) * token_mix(
            h, w_in[l], conv_w[l], conv_b[l], b_igate[l], b_fgate[l], g_qnorm[l], g_knorm[l],
            lambda_qk[l], g_da_out[l], g_ml_out[l], w_out[l], lambda_init)

        h = rmsnorm(x, g_norm[l, 2]) * (1.0 + scale[:, 2, None]) + shift[:, 2, None]
        x = x + 0.5 * (1.0 + gate[:, 2, None]) * swiglu(h, ffn2_w12[l], ffn2_w3[l])
    return x
```

```python
import math
from contextlib import ExitStack
import numpy as np
import ml_dtypes
import concourse.bass as bass
import concourse.mybir as mybir
from concourse.bass_utils import run_bass_kernel_spmd

F32 = mybir.dt.float32
BF16 = mybir.dt.bfloat16
AF = mybir.ActivationFunctionType
ALU = mybir.AluOpType

S = 4096
D = 1024
DFF = 2816
NJ = 22
TT = 512
NT = S // TT
NIN = 3592
EPS = 1e-6
LAMBDA_INIT = 0.8 - 0.6 * math.exp(0.0)
O_BADA, O_GN, O_C, O_CW, O_CB, O_GDA, O_GQ, O_GK, O_GML, NCOL = 0, 72, 96, 104, 136, 144, 145, 146, 147, 151


class Tok:
    def __init__(self, name, acc=False):
        self.name = name
        self.w = {}
        self.r = {}
        self.sem = None
        self.cnt = 0
        self.acc = acc
        self.nobar = False


class Tile:
    def __init__(self, t, name):
        self.t = t
        self.k = Tok(name)


def _tok(b):
    return b.k if isinstance(b, Tile) else b


class Sched:
    def __init__(self, nc, es):
        self.nc = nc
        self.es = es
        self.eng = {"pe": nc.tensor, "dve": nc.vector, "act": nc.scalar, "pool": nc.gpsimd, "sp": nc.sync}
        self.sems = {}
        self.cnt = {}
        for n in ("pe", "dve", "act", "pool"):
            self.sems[n] = es.enter_context(nc.semaphore("c_" + n))
            self.cnt[n] = 0
        self.seen = {n: {} for n in self.eng}
        self.nd = 0
        self.dtoks = []

    def _deps(self, e, reads, writes):
        d = {}

        def add(m, war=False):
            for k, v in m.items():
                if war and k == e:
                    continue
                if v > d.get(k, 0):
                    d[k] = v

        for b in reads:
            add(_tok(b).w)
        for b in writes:
            add(_tok(b).w)
            add(_tok(b).r, war=True)
        return d

    def _wait(self, e, d):
        for k, v in d.items():
            if k == "pe" and e == "pe":
                continue
            if self.seen[e].get(k, 0) < v:
                self.eng[e].wait_ge(self.sems[k], v)
                self.seen[e][k] = v

    def op(self, e, fn, reads=(), writes=()):
        self._wait(e, self._deps(e, reads, writes))
        ins = fn(self.eng[e])
        self.cnt[e] += 1
        ins.then_inc(self.sems[e], 1)
        v = self.cnt[e]
        for b in reads:
            _tok(b).r[e] = v
        for b in writes:
            t = _tok(b)
            t.w = {e: v}
            t.r = {}
        return ins

    def dma(self, q, out, in_, reads=(), writes=(), semtok=None):
        st = _tok(semtok)
        if st.sem is None:
            key = ("d", self.nd)
            self.nd += 1
            self.sems[key] = self.es.enter_context(self.nc.semaphore("d%d" % key[1]))
            st.sem = key
            self.dtoks.append(st)
        self._wait(q, self._deps(q, reads, writes))
        self.eng[q].dma_start(out=out, in_=in_).then_inc(self.sems[st.sem], 16)
        st.cnt += 16
        v = st.cnt
        for b in reads:
            _tok(b).r[st.sem] = v
        for b in writes:
            t = _tok(b)
            if t.acc:
                t.w[st.sem] = v
            else:
                t.w = {st.sem: v}
                t.r = {}

    def barrier(self):
        d = {n: v for n, v in self.cnt.items() if v > 0}
        for t in self.dtoks:
            if not t.nobar:
                d[t.sem] = t.cnt
        for e in self.eng:
            self._wait(e, d)

    def wait_all(self, e, toks):
        d = {}
        for b in toks:
            for k, v in _tok(b).w.items():
                d[k] = max(d.get(k, 0), v)
        self._wait(e, d)


class Pool:
    def __init__(self, items):
        self.items = items
        self.i = 0

    def next(self):
        x = self.items[self.i % len(self.items)]
        self.i += 1
        return x


def build(debug=False):
    nc = bass.Bass("TRN2", target_bir_lowering=False)
    dbg_kind = "ExternalOutput" if debug else "Internal"

    def din(name, shape, dt=F32):
        return nc.dram_tensor(name, list(shape), dt, kind="ExternalInput")

    x_d = din("x", [S, D])
    cols_d = din("cols", [128, NCOL])
    wada_d = din("w_ada", [D, 9 * D])
    f1w12_d = din("ffn1_w12", [D, 2 * DFF])
    f1w3_d = din("ffn1_w3", [DFF, D])
    win_d = din("w_in", [D, NIN])
    wout_d = din("w_out", [D, D])
    f2w12_d = din("ffn2_w12", [D, 2 * DFF])
    f2w3_d = din("ffn2_w3", [DFF, D])
    lamq_d = din("lamq", [1, 256])
    bif_d = din("bif", [1, 8])
    identb_d = din("identb", [128, 128], BF16)
    identf_d = din("identf", [128, 128])
    trib_d = din("trib", [128, 128], BF16)
    trif_d = din("trif", [128, 128])
    onesb_d = din("onesb", [128, 128], BF16)
    onesf_d = din("onesf", [128, 128])
    g64_d = din("g64", [128, 128], BF16)
    ropep_d = din("ropep", [128, 128], BF16)
    cos_d = din("cosT", [128, S])
    sin_d = din("sinT", [128, S])
    out_d = nc.dram_tensor("out", [S, D], F32, kind="ExternalOutput")

    def dscr(name, shape, dt, dbg=False):
        return nc.dram_tensor(name, list(shape), dt, kind=(dbg_kind if dbg else "Internal"))

    f1w12b = dscr("f1w12b", [D, 2 * DFF], BF16)
    f1w3b = dscr("f1w3b", [DFF, D], BF16)
    f2w12b = dscr("f2w12b", [D, 2 * DFF], BF16)
    f2w3b = dscr("f2w3b", [DFF, D], BF16)
    winb = dscr("winb", [D, NIN], BF16)
    woutb = dscr("woutb", [D, D], BF16)
    x1_d = dscr("x1s", [S, D], F32, True)
    daT_d = dscr("daT", [8, 128, S], BF16, True)
    dav_d = dscr("dav", [S, 512], BF16, True)
    mlqk_d = dscr("mlqkT", [8, 128, S + 4], BF16, True)
    mlv_d = dscr("mlv", [S, 512], BF16, True)
    mlo_d = dscr("mlo", [S, 512], BF16, True)
    gat_d = dscr("gat", [S, 8], F32, True)
    yT_d = dscr("yT", [8, 128, S], BF16, True)

    with ExitStack() as es:
        sc = Sched(nc, es)

        def sb(st, name, shape, dt):
            return Tile(st.enter_context(nc.sbuf_tensor("s_" + name, list(shape), dt)), name)

        def ps(st, name, shape, dt=F32):
            if len(shape) == 2:
                shape = [shape[0], 512 if dt == F32 else 1024]
            return Tile(st.enter_context(nc.psum_tensor("p_" + name, list(shape), dt)), name)

        tk = {n: Tok(n, acc=True) for n in
              ["f1w12b", "f1w3b", "f2w12b", "f2w3b", "winb", "woutb", "x1", "daT", "dav", "mlqk", "mlv", "mlo",
               "gat", "yT", "out"]}
        for n in ("f1w12b", "f1w3b", "f2w12b", "f2w3b", "winb", "woutb"):
            tk[n].nobar = True

        identb = sb(es, "identb", [128, 128], BF16)
        identf = sb(es, "identf", [128, 128], F32)
        trib = sb(es, "trib", [128, 128], BF16)
        trif = sb(es, "trif", [128, 128], F32)
        onesb = sb(es, "onesb", [128, 128], BF16)
        onesf = sb(es, "onesf", [128, 128], F32)
        g64 = sb(es, "g64", [128, 128], BF16)
        ropep = sb(es, "ropep", [128, 128], BF16)
        cols = sb(es, "cols", [128, NCOL], F32)
        mc = sb(es, "mc", [128, 72 + 24 + 24 + 8], F32)
        neghalf = sb(es, "neghalf", [128, 512], F32)
        for tl, dd in ((identb, identb_d), (identf, identf_d), (trib, trib_d), (trif, trif_d), (onesb, onesb_d),
                       (onesf, onesf_d), (g64, g64_d), (ropep, ropep_d), (cols, cols_d)):
            sc.dma("sp", tl.t[:], dd.ap(), writes=[tl], semtok=tl)
        sc.op("pool", lambda e: e.memset(neghalf.t[:], -0.5), writes=[neghalf])

        def colap(i):
            return cols.t[:, i:i + 1]

        A_OFF, G_OFF, M_OFF = 72, 96, 120

        def Acol(s, k):
            return mc.t[:, A_OFF + s * 8 + k:A_OFF + s * 8 + k + 1]

        def Bcol(s, k):
            i = (s * 3 + 0) * 8 + k
            return mc.t[:, i:i + 1]

        def Gcol(s, k):
            return mc.t[:, G_OFF + s * 8 + k:G_OFF + s * 8 + k + 1]

        NEGLAM = mc.t[:, M_OFF:M_OFF + 1]
        GDA8 = mc.t[:, M_OFF + 1:M_OFF + 2]

        def cast_w(src, dst, tok, rows, piece=256):
            for r0 in range(0, rows, piece):
                r1 = min(rows, r0 + piece)
                sc.dma("pool", dst.ap()[r0:r1, :], src.ap()[r0:r1, :], writes=[tok], semtok=tok)

        cast_w(f1w12_d, f1w12b, tk["f1w12b"], D)
        cast_w(f1w3_d, f1w3b, tk["f1w3b"], DFF, 704)
        cast_w(win_d, winb, tk["winb"], D)

        with ExitStack() as s0:
            csb = sb(s0, "csb", [128, 8], BF16)
            sc.op("act", lambda e: e.activation(out=csb.t[:], in_=cols.t[:, O_C:O_C + 8], func=AF.Silu),
                  reads=[cols], writes=[csb])
            wa = Pool([sb(s0, "wa%d" % i, [128, 8, 1152], BF16) for i in range(2)])
            modps = ps(s0, "modps", [128, 72])
            for pc in range(8):
                w = wa.next()
                sc.dma("pool", w.t[:], wada_d.ap()[:, pc * 1152:(pc + 1) * 1152].rearrange("(k p) c -> p k c", p=128),
                       writes=[w], semtok=w)
                for qq in range(9):
                    q = pc * 9 + qq
                    for k in range(8):
                        sc.op("pe", lambda e, w=w, qq=qq, k=k, q=q: e.matmul(
                            modps.t[:, q:q + 1], w.t[:, k, qq * 128:(qq + 1) * 128], csb.t[:, k:k + 1],
                            start=(k == 0), stop=(k == 7)), reads=[w, csb], writes=[modps])
            sc.op("dve", lambda e: e.tensor_tensor(mc.t[:, 0:72], modps.t[:, 0:72], cols.t[:, O_BADA:O_BADA + 72], ALU.add),
                  reads=[modps, cols], writes=[mc])
            for s in range(3):
                sc.op("dve", lambda e, s=s: e.scalar_tensor_tensor(
                    mc.t[:, A_OFF + s * 8:A_OFF + s * 8 + 8], mc.t[:, (s * 3 + 1) * 8:(s * 3 + 1) * 8 + 8], 1.0,
                    cols.t[:, O_GN + s * 8:O_GN + s * 8 + 8], ALU.add, ALU.mult), reads=[mc, cols], writes=[mc])
                coef = 1.0 if s == 1 else 0.5
                sc.op("dve", lambda e, s=s, coef=coef: e.tensor_scalar(
                    mc.t[:, G_OFF + s * 8:G_OFF + s * 8 + 8], mc.t[:, (s * 3 + 2) * 8:(s * 3 + 2) * 8 + 8], 1.0, coef,
                    ALU.add, ALU.mult), reads=[mc], writes=[mc])
            lq = sb(s0, "lq", [128, 256], F32)
            sc.dma("sp", lq.t[:], bass.AP(lamq_d, 0, [[0, 128], [1, 256]]), writes=[lq], semtok=lq)
            junk = sb(s0, "junk0", [128, 64], F32)
            l2 = sb(s0, "l2", [128, 4], F32)
            for i in range(2):
                sc.op("dve", lambda e, i=i: e.scalar_tensor_tensor(
                    junk.t[:], lq.t[:, i * 128:i * 128 + 64], 1.0, lq.t[:, i * 128 + 64:i * 128 + 128],
                    ALU.mult, ALU.mult, accum_out=l2.t[:, i:i + 1]), reads=[lq], writes=[junk, l2])
            sc.op("act", lambda e: e.activation(out=l2.t[:, 2:4], in_=l2.t[:, 0:2], func=AF.Exp), reads=[l2], writes=[l2])
            sc.op("dve", lambda e: e.tensor_tensor(l2.t[:, 0:1], l2.t[:, 3:4], l2.t[:, 2:3], ALU.subtract),
                  reads=[l2], writes=[l2])
            sc.op("dve", lambda e: e.tensor_scalar(NEGLAM, l2.t[:, 0:1], -LAMBDA_INIT, None, ALU.add),
                  reads=[l2], writes=[mc])
            sc.op("dve", lambda e: e.tensor_scalar(GDA8, colap(O_GDA), 1.0 - LAMBDA_INIT, None, ALU.mult),
                  reads=[cols], writes=[mc])

        sc.barrier()
        cast_w(wout_d, woutb, tk["woutb"], D)
        cast_w(f2w12_d, f2w12b, tk["f2w12b"], D)
        cast_w(f2w3_d, f2w3b, tk["f2w3b"], DFF, 704)

        def make_ffn_ctx(st, tag):
            c = {}
            c["X4"] = Pool([sb(st, "X4%s%d" % (tag, i), [128, 4, D], F32) for i in range(2)])
            c["xn"] = Pool([sb(st, "xn%s%d" % (tag, i), [128, D], BF16) for i in range(2)])
            c["sq"] = sb(st, "sqj" + tag, [128, D], BF16)
            c["st"] = Pool([sb(st, "stat%s%d" % (tag, i), [128, 4], F32) for i in range(2)])
            c["hT"] = Pool([sb(st, "hT%s%d" % (tag, i), [128, 8, TT], BF16) for i in range(2)])
            c["gT"] = sb(st, "gT" + tag, [128, NJ, TT], BF16)
            c["w12"] = Pool([sb(st, "w12%s%d" % (tag, i), [128, 2, 8, 256], BF16) for i in range(2)])
            c["w3"] = sb(st, "w3" + tag, [128, NJ, D], BF16)
            c["sa"] = Pool([sb(st, "sa%s%d" % (tag, i), [128, TT], F32) for i in range(2)])
            c["oT"] = Pool([sb(st, "oT%s%d" % (tag, i), [128, TT], F32) for i in range(2)])
            c["psA"] = Pool([ps(st, "psA%s%d" % (tag, i), [128, TT]) for i in range(2)])
            c["psB"] = Pool([ps(st, "psB%s%d" % (tag, i), [128, TT]) for i in range(2)])
            c["psO"] = Pool([ps(st, "psO%s%d" % (tag, i), [128, TT]) for i in range(2)])
            c["tp"] = ps(st, "tp" + tag, [128, 8, 128], BF16)
            c["tpo"] = ps(st, "tpo" + tag, [128, 4, 128], F32)
            c["flip"] = 0
            return c

        def norm_T(c, X4, s):
            hT = c["hT"].next()
            for st_ in range(4):
                stt = c["st"].next()
                xs = X4.t[:, st_, :]
                sc.op("dve", lambda e, xs=xs, stt=stt: e.scalar_tensor_tensor(
                    c["sq"].t[:], xs, 1.0, xs, ALU.mult, ALU.mult, accum_out=stt.t[:, 0:1]),
                    reads=[X4], writes=[c["sq"], stt])
                sc.op("dve", lambda e, stt=stt: e.tensor_scalar(stt.t[:, 1:2], stt.t[:, 0:1], 1.0 / D, EPS, ALU.mult, ALU.add),
                      reads=[stt], writes=[stt])
                sc.op("pool", lambda e, stt=stt: e.tensor_tensor(stt.t[:, 2:3], stt.t[:, 1:2], neghalf.t[:, 0:1], ALU.pow),
                      reads=[stt, neghalf], writes=[stt])
                xn = c["xn"].next()
                sc.op("pool", lambda e, xn=xn, xs=xs, stt=stt: e.tensor_scalar(xn.t[:], xs, stt.t[:, 2:3], None, ALU.mult),
                      reads=[X4, stt], writes=[xn])
                for k in range(8):
                    sc.op("pe", lambda e, xn=xn, k=k: e.transpose(c["tp"].t[:, k, :], xn.t[:, k * 128:(k + 1) * 128], identb.t[:]),
                          reads=[xn, identb], writes=[c["tp"]])
                for k in range(8):
                    dst = hT.t[:, k, st_ * 128:(st_ + 1) * 128]
                    src = c["tp"].t[:, k, :]
                    if k % 2 == 0:
                        sc.op("act", lambda e, dst=dst, src=src, k=k: e.activation(
                            out=dst, in_=src, func=AF.Identity, bias=Bcol(s, k), scale=Acol(s, k)),
                            reads=[c["tp"], mc], writes=[hT])
                    else:
                        sc.op("dve", lambda e, dst=dst, src=src, k=k: e.tensor_scalar(
                            dst, src, Acol(s, k), Bcol(s, k), ALU.mult, ALU.add), reads=[c["tp"], mc], writes=[hT])
            return hT

        def phaseB(c, src, W, nj, s, X4):
            for dc in range(8):
                po = c["psO"].next()
                for j in range(nj):
                    sc.op("pe", lambda e, po=po, j=j, dc=dc: e.matmul(
                        po.t[:], W.t[:, j, dc * 128:(dc + 1) * 128], src.t[:, j, :], start=(j == 0), stop=(j == nj - 1)),
                        reads=[W, src], writes=[po])
                oT = c["oT"].next()
                sc.op("act", lambda e, po=po, oT=oT, dc=dc: e.activation(
                    out=oT.t[:], in_=po.t[:], func=AF.Identity, scale=Gcol(s, dc)), reads=[po, mc], writes=[oT])
                for st_ in range(4):
                    sc.op("pe", lambda e, oT=oT, st_=st_: e.transpose(
                        c["tpo"].t[:, st_, :], oT.t[:, st_ * 128:(st_ + 1) * 128], identf.t[:]),
                        reads=[oT, identf], writes=[c["tpo"]])
                xv = X4.t[:, :, dc * 128:(dc + 1) * 128]
                sc.op("dve", lambda e, xv=xv: e.tensor_tensor(xv, xv, c["tpo"].t[:], ALU.add),
                      reads=[X4, c["tpo"]], writes=[X4])

        def ffn(c, hT, w12b, w12tok, s, X4):
            for grp in range(NJ // 2):
                wt = c["w12"].next()
                for ab in range(2):
                    c0 = ab * DFF + grp * 256
                    sc.dma("sp", wt.t[:, ab, :, :], w12b.ap()[:, c0:c0 + 256].rearrange("(k p) c -> p k c", p=128),
                           reads=[w12tok], writes=[wt], semtok=wt)
                for jj in range(2):
                    j = grp * 2 + jj
                    pa = c["psA"].next()
                    pb = c["psB"].next()
                    for ab, pp in ((0, pa), (1, pb)):
                        for k in range(8):
                            sc.op("pe", lambda e, pp=pp, ab=ab, k=k, jj=jj, wt=wt: e.matmul(
                                pp.t[:], wt.t[:, ab, k, jj * 128:(jj + 1) * 128], hT.t[:, k, :], start=(k == 0), stop=(k == 7)),
                                reads=[wt, hT], writes=[pp])
                    sa = c["sa"].next()
                    sc.op("act", lambda e, sa=sa, pa=pa: e.activation(out=sa.t[:], in_=pa.t[:], func=AF.Silu),
                          reads=[pa], writes=[sa])
                    sc.op("dve", lambda e, sa=sa, pb=pb, j=j: e.tensor_tensor(c["gT"].t[:, j, :], sa.t[:], pb.t[:], ALU.mult),
                          reads=[sa, pb], writes=[c["gT"]])
            phaseB(c, c["gT"], c["w3"], NJ, s, X4)

        def load_w3(c, w3b, tok):
            for j0 in range(0, NJ, 11):
                sc.dma("sp", c["w3"].t[:, j0:j0 + 11, :],
                       w3b.ap()[j0 * 128:(j0 + 11) * 128, :].rearrange("(j p) c -> p j c", p=128),
                       reads=[tok], writes=[c["w3"]], semtok=c["w3"])

        with ExitStack() as s1:
            c = make_ffn_ctx(s1, "a")
            load_w3(c, f1w3b, tk["f1w3b"])
            wblk = Pool([sb(s1, "wblk%d" % i, [128, 8, 512], BF16) for i in range(2)])
            wg = sb(s1, "wg", [128, 8, 8], BF16)
            sc.dma("sp", wg.t[:], winb.ap()[:, 3584:3592].rearrange("(k p) c -> p k c", p=128), reads=[tk["winb"]],
                   writes=[wg], semtok=wg)
            cs_t = Pool([sb(s1, "cs%d" % i, [128, 2, TT], F32) for i in range(1)])
            qf = Pool([sb(s1, "qf%d" % i, [128, TT], F32) for i in range(2)])
            sqb = Pool([sb(s1, "sqb%d" % i, [128, TT], BF16) for i in range(2)])
            rs = Pool([sb(s1, "rs%d" % i, [128, TT], F32) for i in range(2)])
            qn = Pool([sb(s1, "qn%d" % i, [128, TT], F32) for i in range(2)])
            qnb = Pool([sb(s1, "qnb%d" % i, [128, TT], BF16) for i in range(2)])
            t1 = Pool([sb(s1, "t1%d" % i, [128, TT], F32) for i in range(2)])
            qr = Pool([sb(s1, "qr%d" % i, [128, TT], BF16) for i in range(3)])
            tmo = Pool([sb(s1, "tmo%d" % i, [128, 4, 512], BF16) for i in range(2)])
            gto = Pool([sb(s1, "gto%d" % i, [128, 4, 8], F32) for i in range(2)])
            zpad = sb(s1, "zpad", [128, 8, 4], BF16)
            sc.op("pool", lambda e: e.memset(zpad.t[:], 0.0), writes=[zpad])
            sc.dma("sp", mlqk_d.ap()[:, :, 0:4].rearrange("g p c -> p g c"), zpad.t[:], reads=[zpad], writes=[tk["mlqk"]],
                   semtok=zpad)
            pw = c["psA"]
            pw2 = c["psB"]

            def load_x(ti):
                X4 = c["X4"].next()
                sc.dma("sp", X4.t[:], x_d.ap()[ti * TT:(ti + 1) * TT, :].rearrange("(s p) d -> p s d", p=128),
                       writes=[X4], semtok=X4)
                return X4

            Xn = load_x(0)
            for ti in range(NT):
                X4 = Xn
                hT = norm_T(c, X4, 0)
                if ti + 1 < NT:
                    Xn = load_x(ti + 1)
                ffn(c, hT, f1w12b, tk["f1w12b"], 0, X4)
                sc.dma("sp", x1_d.ap()[ti * TT:(ti + 1) * TT, :].rearrange("(s p) d -> p s d", p=128), X4.t[:],
                       reads=[X4], writes=[tk["x1"]], semtok=X4)
                h2 = norm_T(c, X4, 1)
                cst = cs_t.next()
                sc.dma("sp", cst.t[:, 0, :], cos_d.ap()[:, ti * TT:(ti + 1) * TT], writes=[cst], semtok=cst)
                sc.dma("sp", cst.t[:, 1, :], sin_d.ap()[:, ti * TT:(ti + 1) * TT], writes=[cst], semtok=cst)
                for blk in range(7):
                    wb_ = wblk.next()
                    sc.dma("sp", wb_.t[:], winb.ap()[:, blk * 512:(blk + 1) * 512].rearrange("(k p) c -> p k c", p=128),
                           reads=[tk["winb"]], writes=[wb_], semtok=wb_)
                    if blk in (0, 1, 3, 4):
                        for cc in range(4):
                            pp = pw.next()
                            for k in range(8):
                                sc.op("pe", lambda e, pp=pp, k=k, cc=cc, wb_=wb_: e.matmul(
                                    pp.t[:], wb_.t[:, k, cc * 128:(cc + 1) * 128], h2.t[:, k, :], start=(k == 0), stop=(k == 7)),
                                    reads=[wb_, h2], writes=[pp])
                            if blk in (3, 4):
                                o = qr.next()
                                sc.op("act", lambda e, o=o, pp=pp: e.activation(out=o.t[:], in_=pp.t[:], func=AF.Identity),
                                      reads=[pp], writes=[o])
                                g = (blk - 3) * 4 + cc
                                sc.dma("sp", mlqk_d.ap()[g, :, 4 + ti * TT:4 + (ti + 1) * TT], o.t[:], reads=[o],
                                       writes=[tk["mlqk"]], semtok=o)
                                continue
                            gcol = colap(O_GQ if blk == 0 else O_GK)
                            f_ = qf.next()
                            sc.op("act", lambda e, f_=f_, pp=pp: e.activation(out=f_.t[:], in_=pp.t[:], func=AF.Identity),
                                  reads=[pp], writes=[f_])
                            sq_ = sqb.next()
                            sc.op("pool", lambda e, sq_=sq_, f_=f_: e.tensor_tensor(sq_.t[:], f_.t[:], f_.t[:], ALU.mult),
                                  reads=[f_], writes=[sq_])
                            p2 = pw2.next()
                            sc.op("pe", lambda e, p2=p2, sq_=sq_: e.matmul(p2.t[:], g64.t[:], sq_.t[:], start=True, stop=True),
                                  reads=[g64, sq_], writes=[p2])
                            r_ = rs.next()
                            sc.op("dve", lambda e, r_=r_, p2=p2: e.tensor_scalar(r_.t[:], p2.t[:], 1.0 / 64, EPS, ALU.mult, ALU.add),
                                  reads=[p2], writes=[r_])
                            sc.op("pool", lambda e, r_=r_: e.tensor_tensor(r_.t[:], r_.t[:], neghalf.t[:], ALU.pow),
                                  reads=[r_, neghalf], writes=[r_])
                            n_ = qn.next()
                            sc.op("dve", lambda e, n_=n_, f_=f_, r_=r_, gcol=gcol: e.scalar_tensor_tensor(
                                n_.t[:], f_.t[:], gcol, r_.t[:], ALU.mult, ALU.mult), reads=[f_, r_, cols], writes=[n_])
                            nb_ = qnb.next()
                            sc.op("act", lambda e, nb_=nb_, n_=n_: e.activation(out=nb_.t[:], in_=n_.t[:], func=AF.Identity),
                                  reads=[n_], writes=[nb_])
                            p3 = pw2.next()
                            sc.op("pe", lambda e, p3=p3, nb_=nb_: e.matmul(p3.t[:], ropep.t[:], nb_.t[:], start=True, stop=True),
                                  reads=[ropep, nb_], writes=[p3])
                            t_ = t1.next()
                            sc.op("pool", lambda e, t_=t_, n_=n_, cst=cst: e.tensor_tensor(t_.t[:], n_.t[:], cst.t[:, 0, :], ALU.mult),
                                  reads=[n_, cst], writes=[t_])
                            sc.op("dve", lambda e, n_=n_, p3=p3, cst=cst: e.tensor_tensor(n_.t[:], p3.t[:], cst.t[:, 1, :], ALU.mult),
                                  reads=[p3, cst], writes=[n_])
                            o = qr.next()
                            sc.op("pool", lambda e, o=o, t_=t_, n_=n_: e.tensor_tensor(o.t[:], t_.t[:], n_.t[:], ALU.add),
                                  reads=[t_, n_], writes=[o])
                            g = blk * 4 + cc
                            sc.dma("sp", daT_d.ap()[g, :, ti * TT:(ti + 1) * TT], o.t[:], reads=[o], writes=[tk["daT"]],
                                   semtok=o)
                    else:
                        o = tmo.next()
                        for st_ in range(4):
                            pp = pw.next()
                            for k in range(8):
                                sc.op("pe", lambda e, pp=pp, k=k, st_=st_, wb_=wb_: e.matmul(
                                    pp.t[:], h2.t[:, k, st_ * 128:(st_ + 1) * 128], wb_.t[:, k, :], start=(k == 0), stop=(k == 7)),
                                    reads=[wb_, h2], writes=[pp])
                            if st_ % 2 == 0:
                                sc.op("act", lambda e, o=o, pp=pp, st_=st_: e.activation(out=o.t[:, st_, :], in_=pp.t[:], func=AF.Identity),
                                      reads=[pp], writes=[o])
                            else:
                                sc.op("dve", lambda e, o=o, pp=pp, st_=st_: e.tensor_copy(o.t[:, st_, :], pp.t[:]),
                                      reads=[pp], writes=[o])
                        dd, tkn = {2: (dav_d, "dav"), 5: (mlv_d, "mlv"), 6: (mlo_d, "mlo")}[blk]
                        sc.dma("sp", dd.ap()[ti * TT:(ti + 1) * TT, :].rearrange("(s p) c -> p s c", p=128), o.t[:],
                               reads=[o], writes=[tk[tkn]], semtok=o)
                go = gto.next()
                pp = pw2.next()
                for st_ in range(4):
                    for k in range(8):
                        sc.op("pe", lambda e, pp=pp, k=k, st_=st_: e.matmul(
                            pp.t[:, st_ * 8:(st_ + 1) * 8], h2.t[:, k, st_ * 128:(st_ + 1) * 128], wg.t[:, k, :],
                            start=(k == 0), stop=(k == 7)), reads=[wg, h2], writes=[pp])
                sc.op("dve", lambda e, go=go, pp=pp: e.tensor_copy(go.t[:].rearrange("p s c -> p (s c)"), pp.t[:, 0:32]),
                      reads=[pp], writes=[go])
                sc.dma("sp", gat_d.ap()[ti * TT:(ti + 1) * TT, :].rearrange("(s p) c -> p s c", p=128), go.t[:], reads=[go],
                       writes=[tk["gat"]], semtok=go)

        sc.barrier()
        with ExitStack() as s2:
            QT = Pool([sb(s2, "QT%d" % i, [128, S], BF16) for i in range(2)])
            KT = Pool([sb(s2, "KT%d" % i, [128, S], BF16) for i in range(2)])
            VV = Pool([sb(s2, "VV%d" % i, [128, 32, 128], BF16) for i in range(2)])
            yTh = Pool([sb(s2, "yTh%d" % i, [128, S], BF16) for i in range(2)])
            Et = Pool([sb(s2, "Et%d" % i, [128, TT], BF16) for i in range(4)])
            wk = Pool([ps(s2, "wk%d" % i, [128, TT]) for i in range(4)])
            oacc = [ps(s2, "oacc%d" % i, [128, TT]) for i in range(2)]
            dacc = [ps(s2, "dacc%d" % i, [128, TT]) for i in range(2)]
            osb = [sb(s2, "osb%d" % i, [128, TT], F32) for i in range(2)]
            rdn = [sb(s2, "rdn%d" % i, [128, TT], F32) for i in range(2)]
            ddt = sb(s2, "ddt", [128, TT], F32)
            sq2 = sb(s2, "sq2", [128, TT], BF16)
            rs2 = sb(s2, "rs2", [128, TT], F32)
            for h in range(4):
                q_, k_, v_, y_ = QT.next(), KT.next(), VV.next(), yTh.next()
                sc.dma("sp", q_.t[:], daT_d.ap()[h], reads=[tk["daT"]], writes=[q_], semtok=q_)
                sc.dma("sp", k_.t[:], daT_d.ap()[4 + h], reads=[tk["daT"]], writes=[k_], semtok=k_)
                sc.dma("sp", v_.t[:], dav_d.ap()[:, h * 128:(h + 1) * 128].rearrange("(t p) c -> p t c", p=128),
                       reads=[tk["dav"]], writes=[v_], semtok=v_)
                for qb in range(8):
                    nkb = 4 * (qb + 1)
                    for kb in range(nkb):
                        r = kb - 4 * qb
                        q0 = 128 * r if r > 0 else 0
                        for cpt in range(2):
                            stp = wk.next()
                            lo, hi = cpt * 64, (cpt + 1) * 64
                            sc.op("pe", lambda e, stp=stp, lo=lo, hi=hi, kb=kb, q0=q0, qb=qb: e.matmul(
                                stp.t[:, q0:TT], k_.t[lo:hi, kb * 128:(kb + 1) * 128], q_.t[lo:hi, qb * TT + q0:(qb + 1) * TT],
                                start=True, stop=True), reads=[k_, q_], writes=[stp])
                            E = Et.next()
                            sc.op("act", lambda e, E=E, stp=stp, q0=q0: e.activation(
                                out=E.t[:, q0:TT], in_=stp.t[:, q0:TT], func=AF.Exp, scale=0.125), reads=[stp], writes=[E])
                            if r >= 0:
                                sc.op("dve", lambda e, E=E, q0=q0: e.tensor_tensor(
                                    E.t[:, q0:q0 + 128], E.t[:, q0:q0 + 128], trib.t[:], ALU.mult), reads=[E, trib], writes=[E])
                            sc.op("pe", lambda e, E=E, cpt=cpt, kb=kb, q0=q0, nkb=nkb: e.matmul(
                                oacc[cpt].t[:, q0:TT], v_.t[:, kb, :], E.t[:, q0:TT], start=(kb == 0), stop=(kb == nkb - 1),
                                skip_group_check=True), reads=[v_, E], writes=[oacc[cpt]])
                            sc.op("pe", lambda e, E=E, cpt=cpt, kb=kb, q0=q0, nkb=nkb: e.matmul(
                                dacc[cpt].t[:, q0:TT], onesb.t[:], E.t[:, q0:TT], start=(kb == 0), stop=(kb == nkb - 1),
                                skip_group_check=True), reads=[onesb, E], writes=[dacc[cpt]])
                    for cpt in range(2):
                        sc.op("act", lambda e, cpt=cpt: e.activation(out=osb[cpt].t[:], in_=oacc[cpt].t[:], func=AF.Identity),
                              reads=[oacc[cpt]], writes=[osb[cpt]])
                        sc.op("dve", lambda e, cpt=cpt: e.reciprocal(rdn[cpt].t[:], dacc[cpt].t[:]),
                              reads=[dacc[cpt]], writes=[rdn[cpt]])
                        sc.op("pool", lambda e, cpt=cpt: e.tensor_tensor(osb[cpt].t[:], osb[cpt].t[:], rdn[cpt].t[:], ALU.mult),
                              reads=[osb[cpt], rdn[cpt]], writes=[osb[cpt]])
                    sc.op("dve", lambda e: e.scalar_tensor_tensor(ddt.t[:], osb[1].t[:], NEGLAM, osb[0].t[:], ALU.mult, ALU.add),
                          reads=[osb[0], osb[1], mc], writes=[ddt])
                    sc.op("pool", lambda e: e.tensor_tensor(sq2.t[:], ddt.t[:], ddt.t[:], ALU.mult), reads=[ddt], writes=[sq2])
                    p2 = wk.next()
                    sc.op("pe", lambda e, p2=p2: e.matmul(p2.t[:], onesb.t[:], sq2.t[:], start=True, stop=True),
                          reads=[onesb, sq2], writes=[p2])
                    sc.op("dve", lambda e, p2=p2: e.tensor_scalar(rs2.t[:], p2.t[:], 1.0 / 128, EPS, ALU.mult, ALU.add),
                          reads=[p2], writes=[rs2])
                    sc.op("pool", lambda e: e.tensor_tensor(rs2.t[:], rs2.t[:], neghalf.t[:], ALU.pow),
                          reads=[rs2, neghalf], writes=[rs2])
                    sc.op("dve", lambda e, qb=qb, y_=y_: e.scalar_tensor_tensor(
                        y_.t[:, qb * TT:(qb + 1) * TT], ddt.t[:], GDA8, rs2.t[:], ALU.mult, ALU.mult),
                        reads=[ddt, rs2, mc], writes=[y_])
                sc.dma("sp", yT_d.ap()[h], y_.t[:], reads=[y_], writes=[tk["yT"]], semtok=y_)

        sc.barrier()
        with ExitStack() as s3:
            LNS = math.log(128.0 ** -0.5)
            gt = sb(s3, "gt", [128, 32, 8], F32)
            sc.dma("sp", gt.t[:], gat_d.ap().rearrange("(c p) g -> p c g", p=128), reads=[tk["gat"]], writes=[gt], semtok=gt)
            bif = sb(s3, "bif", [128, 8], F32)
            sc.dma("sp", bif.t[:], bass.AP(bif_d, 0, [[0, 128], [1, 8]]), writes=[bif], semtok=bif)
            lnsc = sb(s3, "lnsc", [128, 1], F32)
            sc.op("pool", lambda e: e.memset(lnsc.t[:], LNS), writes=[lnsc])
            raw = Pool([sb(s3, "raw%d" % i, [128, S + 4], BF16) for i in range(2)])
            qT = sb(s3, "mqT", [128, S], BF16)
            kT = sb(s3, "mkT", [128, S], BF16)
            kS = sb(s3, "mkS", [128, 32, 128], BF16)
            va = sb(s3, "mva", [128, 32, 129], BF16)
            sc.op("pool", lambda e: e.memset(va.t[:, :, 128:129], 1.0), writes=[va])
            og = sb(s3, "mog", [128, 32, 128], BF16)
            yh = sb(s3, "myh", [128, S], BF16)
            cacc = Pool([sb(s3, "cacc%d" % i, [128, 1024], F32) for i in range(2)])
            gs = sb(s3, "gs", [128, 32 * 10], F32)
            C32 = sb(s3, "C32", [128, 129], F32)
            Cb = Pool([sb(s3, "Cb%d" % i, [128, 129], BF16) for i in range(2)])
            WT = Pool([sb(s3, "WT%d" % i, [128, 128], BF16) for i in range(2)])
            ytm = Pool([sb(s3, "ytm%d" % i, [128, 128], BF16) for i in range(2)])
            sm = Pool([sb(s3, "sm%d" % i, [128, 8], F32) for i in range(4)])
            jk = sb(s3, "jk", [128, 128], BF16)
            pST = Pool([ps(s3, "pST%d" % i, [128, 128]) for i in range(2)])
            pZ = Pool([ps(s3, "pZ%d" % i, [128, 129]) for i in range(2)])
            pU = Pool([ps(s3, "pU%d" % i, [128, 129]) for i in range(2)])
            pT = ps(s3, "pT", [128, 8, 128], BF16)
            pG = ps(s3, "pG", [128, 64])

            def G(i, c0=0, c1=32):
                return gs.t[:, i * 32 + c0:i * 32 + c1]

            for h in range(4):
                sc.op("dve", lambda e, h=h: e.tensor_scalar(G(0), gt.t[:, :, h], bif.t[:, h:h + 1], None, ALU.add),
                      reads=[gt, bif], writes=[gs])
                sc.op("dve", lambda e, h=h: e.tensor_scalar(G(1), gt.t[:, :, 4 + h], bif.t[:, 4 + h:5 + h], None, ALU.add),
                      reads=[gt, bif], writes=[gs])
                sc.op("act", lambda e: e.activation(out=G(2), in_=G(1), func=AF.Exp, scale=-1.0), reads=[gs], writes=[gs])
                sc.op("act", lambda e: e.activation(out=G(2), in_=G(2), func=AF.Ln, bias=1.0), reads=[gs], writes=[gs])
                sc.op("pe", lambda e: e.matmul(pG.t[:, 0:32], trif.t[:], G(2), start=True, stop=True),
                      reads=[trif, gs], writes=[pG])
                sc.op("pe", lambda e: e.matmul(pG.t[:, 32:64], onesf.t[:], G(2), start=True, stop=True),
                      reads=[onesf, gs], writes=[pG])
                sc.op("dve", lambda e: e.tensor_tensor(G(3), pG.t[:, 0:32], G(0), ALU.add), reads=[pG, gs], writes=[gs])
                sc.op("act", lambda e: e.activation(out=G(4), in_=G(3), func=AF.Exp, bias=lnsc.t[:]), reads=[gs, lnsc], writes=[gs])
                sc.op("act", lambda e: e.activation(out=G(5), in_=pG.t[:, 0:32], func=AF.Exp), reads=[pG], writes=[gs])
                sc.op("dve", lambda e: e.tensor_tensor(G(6), G(3), pG.t[:, 32:64], ALU.subtract), reads=[pG, gs], writes=[gs])
                sc.op("act", lambda e: e.activation(out=G(7), in_=G(6), func=AF.Exp, bias=lnsc.t[:]), reads=[gs, lnsc], writes=[gs])
                sc.op("act", lambda e: e.activation(out=G(8), in_=pG.t[:, 32:64], func=AF.Exp, scale=-1.0), reads=[pG], writes=[gs])
                for which, dst in ((0, qT), (1, kT)):
                    rw = raw.next()
                    ch = which * 4 + h
                    sc.dma("sp", rw.t[:], mlqk_d.ap()[ch], reads=[tk["mlqk"]], writes=[rw], semtok=rw)
                    for pc in range(4):
                        o0 = pc * 1024
                        ca = cacc.next()
                        sc.op("dve", lambda e, ca=ca, rw=rw, o0=o0, ch=ch: e.tensor_scalar(
                            ca.t[:], rw.t[:, 1 + o0:1 + o0 + 1024], colap(O_CW + ch), colap(O_CB + ch), ALU.mult, ALU.add),
                            reads=[rw, cols], writes=[ca])
                        for j in range(1, 4):
                            sc.op("dve", lambda e, ca=ca, rw=rw, o0=o0, ch=ch, j=j: e.scalar_tensor_tensor(
                                ca.t[:], rw.t[:, 1 + o0 + j:1 + o0 + j + 1024], colap(O_CW + j * 8 + ch), ca.t[:], ALU.mult, ALU.add),
                                reads=[rw, cols, ca], writes=[ca])
                        sc.op("act", lambda e, ca=ca, dst=dst, o0=o0: e.activation(out=dst.t[:, o0:o0 + 1024], in_=ca.t[:], func=AF.Silu),
                              reads=[ca], writes=[dst])
                for c8 in range(4):
                    for i in range(8):
                        cch = c8 * 8 + i
                        sc.op("pe", lambda e, i=i, cch=cch: e.transpose(pT.t[:, i, :], kT.t[:, cch * 128:(cch + 1) * 128], identb.t[:]),
                              reads=[kT, identb], writes=[pT])
                    for i in range(8):
                        cch = c8 * 8 + i
                        if i % 2 == 0:
                            sc.op("dve", lambda e, i=i, cch=cch: e.tensor_scalar(kS.t[:, cch, :], pT.t[:, i, :], G(7, cch, cch + 1), None, ALU.mult),
                                  reads=[pT, gs], writes=[kS])
                        else:
                            sc.op("act", lambda e, i=i, cch=cch: e.activation(out=kS.t[:, cch, :], in_=pT.t[:, i, :], func=AF.Copy, scale=G(7, cch, cch + 1)),
                                  reads=[pT, gs], writes=[kS])
                sc.dma("sp", va.t[:, :, 0:128], mlv_d.ap()[:, h * 128:(h + 1) * 128].rearrange("(c p) d -> p c d", p=128),
                       reads=[tk["mlv"]], writes=[va], semtok=va)
                sc.dma("sp", og.t[:], mlo_d.ap()[:, h * 128:(h + 1) * 128].rearrange("(c p) d -> p c d", p=128),
                       reads=[tk["mlo"]], writes=[og], semtok=og)
                sc.op("act", lambda e: e.activation(out=og.t[:], in_=og.t[:], func=AF.Sigmoid), reads=[og], writes=[og])
                cb_prev = None
                for cch in range(32):
                    cs_ = slice(cch * 128, (cch + 1) * 128)
                    st_p = pST.next()
                    sc.op("pe", lambda e, st_p=st_p, cs_=cs_: e.matmul(st_p.t[:, 0:128], kT.t[:, cs_], qT.t[:, cs_], start=True, stop=True),
                          reads=[kT, qT], writes=[st_p])
                    wt_ = WT.next()
                    sc.op("dve", lambda e, wt_=wt_, st_p=st_p, cch=cch: e.scalar_tensor_tensor(
                        wt_.t[:], st_p.t[:, 0:128], G(4, cch, cch + 1), trib.t[:], ALU.mult, ALU.mult),
                        reads=[st_p, gs, trib], writes=[wt_])
                    z = pZ.next()
                    sc.op("pe", lambda e, z=z, wt_=wt_, cch=cch: e.matmul(z.t[:, 0:129], wt_.t[:], va.t[:, cch, :], start=True, stop=(cch == 0)),
                          reads=[wt_, va], writes=[z])
                    if cch > 0:
                        sc.op("pe", lambda e, z=z, cs_=cs_, cb_prev=cb_prev: e.matmul(z.t[:, 0:129], qT.t[:, cs_], cb_prev.t[:], start=False, stop=True),
                              reads=[qT, cb_prev], writes=[z])
                    if cch < 31:
                        u = pU.next()
                        sc.op("pe", lambda e, u=u, cch=cch: e.matmul(u.t[:, 0:129], kS.t[:, cch, :], va.t[:, cch, :], start=True, stop=True),
                              reads=[kS, va], writes=[u])
                        if cch == 0:
                            sc.op("dve", lambda e, u=u: e.tensor_copy(C32.t[:], u.t[:, 0:129]), reads=[u], writes=[C32])
                        else:
                            sc.op("dve", lambda e, u=u, cch=cch: e.scalar_tensor_tensor(
                                C32.t[:], C32.t[:], G(8, cch, cch + 1), u.t[:, 0:129], ALU.mult, ALU.add), reads=[u, C32, gs], writes=[C32])
                        cb_prev = Cb.next()
                        sc.op("act", lambda e, cb_prev=cb_prev: e.activation(out=cb_prev.t[:], in_=C32.t[:], func=AF.Copy),
                              reads=[C32], writes=[cb_prev])
                    m = sm.next()
                    sc.op("dve", lambda e, m=m, z=z: e.tensor_scalar(m.t[:, 7:8], z.t[:, 128:129], -1.0, None, ALU.mult),
                          reads=[z], writes=[m])
                    sc.op("dve", lambda e, m=m, z=z, cch=cch: e.scalar_tensor_tensor(
                        m.t[:, 0:1], z.t[:, 128:129], G(5, cch, cch + 1), m.t[:, 7:8], ALU.max, ALU.max), reads=[z, gs, m], writes=[m])
                    sc.op("dve", lambda e, m=m: e.reciprocal(m.t[:, 1:2], m.t[:, 0:1]), reads=[m], writes=[m])
                    sc.op("act", lambda e, m=m, z=z: e.activation(out=jk.t[:], in_=z.t[:, 0:128], func=AF.Square, accum_out=m.t[:, 2:3]),
                          reads=[z], writes=[jk, m])
                    sc.op("dve", lambda e, m=m: e.scalar_tensor_tensor(m.t[:, 3:4], m.t[:, 2:3], m.t[:, 1:2], m.t[:, 1:2], ALU.mult, ALU.mult),
                          reads=[m], writes=[m])
                    sc.op("dve", lambda e, m=m: e.tensor_scalar(m.t[:, 4:5], m.t[:, 3:4], 1.0 / 128, EPS, ALU.mult, ALU.add),
                          reads=[m], writes=[m])
                    sc.op("pool", lambda e, m=m: e.tensor_tensor(m.t[:, 5:6], m.t[:, 4:5], neghalf.t[:, 0:1], ALU.pow),
                          reads=[m, neghalf], writes=[m])
                    sc.op("pool", lambda e, m=m: e.tensor_tensor(m.t[:, 6:7], m.t[:, 5:6], m.t[:, 1:2], ALU.mult), reads=[m], writes=[m])
                    yt = ytm.next()
                    sc.op("dve", lambda e, yt=yt, z=z, m=m, cch=cch: e.scalar_tensor_tensor(
                        yt.t[:], z.t[:, 0:128], m.t[:, 6:7], og.t[:, cch, :], ALU.mult, ALU.mult), reads=[z, m, og], writes=[yt])
                    i8 = cch % 8
                    sc.op("pe", lambda e, yt=yt, i8=i8: e.transpose(pT.t[:, i8, :], yt.t[:], identb.t[:]), reads=[yt, identb], writes=[pT])
                    sc.op("act", lambda e, i8=i8, cs_=cs_, h=h: e.activation(out=yh.t[:, cs_], in_=pT.t[:, i8, :], func=AF.Copy, scale=colap(O_GML + h)),
                          reads=[pT, cols], writes=[yh])
                sc.dma("sp", yT_d.ap()[4 + h], yh.t[:], reads=[yh], writes=[tk["yT"]], semtok=yh)

        sc.barrier()
        with ExitStack() as s4:
            c = make_ffn_ctx(s4, "b")
            load_w3(c, f2w3b, tk["f2w3b"])
            wo = sb(s4, "wo", [128, 8, D], BF16)
            sc.dma("sp", wo.t[:], woutb.ap().rearrange("(m p) c -> p m c", p=128), reads=[tk["woutb"]], writes=[wo], semtok=wo)
            yTt = Pool([sb(s4, "yTt%d" % i, [128, 8, TT], BF16) for i in range(2)])

            def load4(ti):
                X4 = c["X4"].next()
                sc.dma("sp", X4.t[:], x1_d.ap()[ti * TT:(ti + 1) * TT, :].rearrange("(s p) d -> p s d", p=128),
                       reads=[tk["x1"]], writes=[X4], semtok=X4)
                y = yTt.next()
                sc.dma("sp", y.t[:], yT_d.ap()[:, :, ti * TT:(ti + 1) * TT].rearrange("m p t -> p m t"), reads=[tk["yT"]],
                       writes=[y], semtok=y)
                return X4, y

            nxt = load4(0)
            for ti in range(NT):
                X4, y = nxt
                phaseB(c, y, wo, 8, 1, X4)
                if ti + 1 < NT:
                    nxt = load4(ti + 1)
                hT = norm_T(c, X4, 2)
                ffn(c, hT, f2w12b, tk["f2w12b"], 2, X4)
                sc.dma("sp", out_d.ap()[ti * TT:(ti + 1) * TT, :].rearrange("(s p) d -> p s d", p=128), X4.t[:],
                       reads=[X4], writes=[tk["out"]], semtok=X4)
            sc.wait_all("sp", [tk["out"]])
    return nc


def _consts():
    bf = ml_dtypes.bfloat16
    i = np.arange(128)
    tri = (i[:, None] <= i[None, :]).astype(np.float32)
    g64 = ((i[:, None] // 64) == (i[None, :] // 64)).astype(np.float32)
    P = np.zeros((128, 128), np.float32)
    for d in range(128):
        m = d % 64
        if m < 8:
            P[d + 8, d] = -1.0
        elif m < 16:
            P[d - 8, d] = 1.0
    inv_freq = (500000.0 ** (-(np.arange(0, 16, 2).astype(np.float32)) / np.float32(16))).astype(np.float32)
    pos = np.arange(S, dtype=np.float32)
    ang = (pos[:, None] * inv_freq[None, :]).astype(np.float32)
    cosT = np.ones((128, S), np.float32)
    sinT = np.zeros((128, S), np.float32)
    for d in range(128):
        m = d % 64
        if m < 16:
            cosT[d] = np.cos(ang[:, m % 8].astype(np.float64)).astype(np.float32)
            sinT[d] = np.sin(ang[:, m % 8].astype(np.float64)).astype(np.float32)
    return {
        "identb": np.eye(128, dtype=np.float32).astype(bf), "identf": np.eye(128, dtype=np.float32),
        "trib": tri.astype(bf), "trif": tri, "onesb": np.ones((128, 128), np.float32).astype(bf),
        "onesf": np.ones((128, 128), np.float32), "g64": g64.astype(bf), "ropep": P.astype(bf),
        "cosT": cosT, "sinT": sinT,
    }


def _in_maps(inp):
    f = lambda a: np.ascontiguousarray(np.asarray(a, dtype=np.float32))
    consts = _consts()
    col = lambda v, n: f(v).reshape(n, 128).T
    g_q = f(inp["g_qnorm"])[0]
    g_k = f(inp["g_knorm"])[0]
    shared = {
        "w_ada": f(inp["w_ada"])[0], "ffn1_w12": f(inp["ffn1_w12"])[0], "ffn1_w3": f(inp["ffn1_w3"])[0],
        "w_in": f(inp["w_in"])[0], "w_out": f(inp["w_out"])[0], "ffn2_w12": f(inp["ffn2_w12"])[0],
        "ffn2_w3": f(inp["ffn2_w3"])[0],
        "lamq": f(inp["lambda_qk"])[0].reshape(1, 256),
        "bif": np.concatenate([f(inp["b_igate"])[0], f(inp["b_fgate"])[0]]).reshape(1, 8),
    }
    shared.update(consts)
    cw = f(inp["conv_w"])[0].reshape(4, 8, 128).transpose(2, 0, 1).reshape(128, 32)
    maps = []
    x = f(inp["x"])
    c = f(inp["c"])
    for b in range(8):
        cols = np.concatenate([
            col(inp["b_ada"][0], 72), col(inp["g_norm"][0], 24), col(c[b], 8), cw, col(inp["conv_b"][0], 8),
            f(inp["g_da_out"])[0].reshape(128, 1), np.concatenate([g_q, g_q]).reshape(128, 1),
            np.concatenate([g_k, g_k]).reshape(128, 1), col(inp["g_ml_out"][0], 4)], axis=1)
        m = dict(shared)
        m["x"] = x[b]
        m["cols"] = np.ascontiguousarray(cols.astype(np.float32))
        maps.append(m)
    return maps


def kernel(**inputs):
    nc = build()
    res = run_bass_kernel_spmd(nc, _in_maps(inputs), core_ids=list(range(8)))
    return np.stack([np.asarray(r["out"], dtype=np.float32) for r in res.results], axis=0)
```

```python
import math
from contextlib import ExitStack
import numpy as np
import ml_dtypes
import concourse.bass as bass
import concourse.mybir as mybir
from concourse.bass_utils import run_bass_kernel_spmd

F32 = mybir.dt.float32
BF16 = mybir.dt.bfloat16
AF = mybir.ActivationFunctionType
ALU = mybir.AluOpType

S = 4096
D = 1024
DFF = 2816
NJ = 22
TT = 512
NT = S // TT
NIN = 3592
EPS = 1e-6
LAMBDA_INIT = 0.8 - 0.6 * math.exp(0.0)
O_BADA, O_GN, O_C, O_CW, O_CB, O_GDA, O_GQ, O_GK, O_GML, NCOL = 0, 72, 96, 104, 136, 144, 145, 146, 147, 151


class Tok:
    def __init__(self, name, acc=False):
        self.name = name
        self.w = {}
        self.r = {}
        self.sem = None
        self.cnt = 0
        self.acc = acc
        self.nobar = False


class Tile:
    def __init__(self, t, name):
        self.t = t
        self.k = Tok(name)


def _tok(b):
    return b.k if isinstance(b, Tile) else b


class Sched:
    def __init__(self, nc, es):
        self.nc = nc
        self.es = es
        self.eng = {"pe": nc.tensor, "dve": nc.vector, "act": nc.scalar, "pool": nc.gpsimd, "sp": nc.sync}
        self.sems = {}
        self.cnt = {}
        for n in ("pe", "dve", "act", "pool"):
            self.sems[n] = es.enter_context(nc.semaphore("c_" + n))
            self.cnt[n] = 0
        self.seen = {n: {} for n in self.eng}
        self.nd = 0
        self.dtoks = []

    def _deps(self, e, reads, writes):
        d = {}

        def add(m, war=False):
            for k, v in m.items():
                if v > d.get(k, 0):
                    d[k] = v

        for b in reads:
            add(_tok(b).w)
        for b in writes:
            add(_tok(b).w)
            add(_tok(b).r, war=True)
        return d

    def _wait(self, e, d):
        for k, v in d.items():
            if k == "pe" and e == "pe":
                continue
            if self.seen[e].get(k, 0) < v:
                self.eng[e].wait_ge(self.sems[k], v)
                self.seen[e][k] = v

    def op(self, e, fn, reads=(), writes=()):
        self._wait(e, self._deps(e, reads, writes))
        ins = fn(self.eng[e])
        self.cnt[e] += 1
        ins.then_inc(self.sems[e], 1)
        v = self.cnt[e]
        for b in reads:
            _tok(b).r[e] = v
        for b in writes:
            t = _tok(b)
            t.w = {e: v}
            t.r = {}
        return ins

    def dma(self, q, out, in_, reads=(), writes=(), semtok=None):
        st = _tok(semtok)
        kind = "sw" if q == "pool" else "hw"
        if st.sem is None:
            st.sem = {}
            st.cnt = {}
            self.dtoks.append(st)
        if kind not in st.sem:
            key = ("d", self.nd)
            self.nd += 1
            self.sems[key] = self.es.enter_context(self.nc.semaphore("d%d" % key[1]))
            st.sem[kind] = key
            st.cnt[kind] = 0
        key = st.sem[kind]
        self._wait(q, self._deps(q, reads, writes))
        self.eng[q].dma_start(out=out, in_=in_).then_inc(self.sems[key], 16)
        st.cnt[kind] += 16
        v = st.cnt[kind]
        for b in reads:
            _tok(b).r[key] = v
        for b in writes:
            t = _tok(b)
            if t.acc:
                t.w[key] = v
            else:
                t.w = {key: v}
                t.r = {}

    def barrier(self):
        d = {n: v for n, v in self.cnt.items() if v > 0}
        for t in self.dtoks:
            if not t.nobar:
                for kind, key in t.sem.items():
                    d[key] = t.cnt[kind]
        for e in self.eng:
            self._wait(e, d)

    def wait_all(self, e, toks):
        d = {}
        for b in toks:
            for k, v in _tok(b).w.items():
                d[k] = max(d.get(k, 0), v)
        self._wait(e, d)


class Pool:
    def __init__(self, items):
        self.items = items
        self.i = 0

    def next(self):
        x = self.items[self.i % len(self.items)]
        self.i += 1
        return x


def build(debug=False):
    nc = bass.Bass("TRN2", target_bir_lowering=False)
    dbg_kind = "ExternalOutput" if debug else "Internal"

    def din(name, shape, dt=F32):
        return nc.dram_tensor(name, list(shape), dt, kind="ExternalInput")

    x_d = din("x", [S, D])
    cols_d = din("cols", [128, NCOL])
    wada_d = din("w_ada", [D, 9 * D])
    f1w12_d = din("ffn1_w12", [D, 2 * DFF])
    f1w3_d = din("ffn1_w3", [DFF, D])
    win_d = din("w_in", [D, NIN])
    wout_d = din("w_out", [D, D])
    f2w12_d = din("ffn2_w12", [D, 2 * DFF])
    f2w3_d = din("ffn2_w3", [DFF, D])
    lamq_d = din("lamq", [1, 256])
    bif_d = din("bif", [1, 8])
    identb_d = din("identb", [128, 128], BF16)
    identf_d = din("identf", [128, 128])
    trib_d = din("trib", [128, 128], BF16)
    trif_d = din("trif", [128, 128])
    onesb_d = din("onesb", [128, 128], BF16)
    onesf_d = din("onesf", [128, 128])
    g64_d = din("g64", [128, 128], BF16)
    ropep_d = din("ropep", [128, 128], BF16)
    cos_d = din("cosT", [128, S])
    sin_d = din("sinT", [128, S])
    out_d = nc.dram_tensor("out", [S, D], F32, kind="ExternalOutput")

    def dscr(name, shape, dt, dbg=False):
        return nc.dram_tensor(name, list(shape), dt, kind=(dbg_kind if dbg else "Internal"))

    f1w12b = dscr("f1w12b", [D, 2 * DFF], BF16)
    f1w3b = dscr("f1w3b", [DFF, D], BF16)
    f2w12b = dscr("f2w12b", [D, 2 * DFF], BF16)
    f2w3b = dscr("f2w3b", [DFF, D], BF16)
    winb = dscr("winb", [D, NIN], BF16)
    woutb = dscr("woutb", [D, D], BF16)
    x1_d = dscr("x1s", [S, D], F32, True)
    daT_d = dscr("daT", [8, 128, S], BF16, True)
    dav_d = dscr("dav", [S, 512], BF16, True)
    mlqk_d = dscr("mlqkT", [8, 128, S + 4], BF16, True)
    mlv_d = dscr("mlv", [S, 512], BF16, True)
    mlo_d = dscr("mlo", [S, 512], BF16, True)
    gat_d = dscr("gat", [S, 8], F32, True)
    yT_d = dscr("yT", [8, 128, S], BF16, True)

    with ExitStack() as es:
        sc = Sched(nc, es)

        def sb(st, name, shape, dt):
            return Tile(st.enter_context(nc.sbuf_tensor("s_" + name, list(shape), dt)), name)

        def ps(st, name, shape, dt=F32):
            if len(shape) == 2:
                shape = [shape[0], 512 if dt == F32 else 1024]
            return Tile(st.enter_context(nc.psum_tensor("p_" + name, list(shape), dt)), name)

        tk = {n: Tok(n, acc=True) for n in
              ["f1w12b", "f1w3b", "f2w12b", "f2w3b", "winb", "woutb", "x1", "daT", "dav", "mlqk", "mlv", "mlo",
               "gat", "yT", "out"]}
        for n in ("f1w12b", "f1w3b", "f2w12b", "f2w3b", "winb", "woutb"):
            tk[n].nobar = True

        identb = sb(es, "identb", [128, 128], BF16)
        identf = sb(es, "identf", [128, 128], F32)
        trib = sb(es, "trib", [128, 128], BF16)
        trif = sb(es, "trif", [128, 128], F32)
        onesb = sb(es, "onesb", [128, 128], BF16)
        onesf = sb(es, "onesf", [128, 128], F32)
        g64 = sb(es, "g64", [128, 128], BF16)
        ropep = sb(es, "ropep", [128, 128], BF16)
        cols = sb(es, "cols", [128, NCOL], F32)
        mc = sb(es, "mc", [128, 72 + 24 + 24 + 8], F32)
        neghalf = sb(es, "neghalf", [128, 8], F32)
        for tl, dd in ((identb, identb_d), (identf, identf_d), (trib, trib_d), (trif, trif_d), (onesb, onesb_d),
                       (onesf, onesf_d), (g64, g64_d), (ropep, ropep_d), (cols, cols_d)):
            sc.dma("sp", tl.t[:], dd.ap(), writes=[tl], semtok=tl)
        sc.op("pool", lambda e: e.memset(neghalf.t[:], -0.5), writes=[neghalf])
        epsc = sb(es, "epsc", [128, 1], F32)
        sc.op("pool", lambda e: e.memset(epsc.t[:], EPS), writes=[epsc])

        def colap(i):
            return cols.t[:, i:i + 1]

        A_OFF, G_OFF, M_OFF = 72, 96, 120

        def Acol(s, k):
            return mc.t[:, A_OFF + s * 8 + k:A_OFF + s * 8 + k + 1]

        def Bcol(s, k):
            i = (s * 3 + 0) * 8 + k
            return mc.t[:, i:i + 1]

        def Gcol(s, k):
            return mc.t[:, G_OFF + s * 8 + k:G_OFF + s * 8 + k + 1]

        NEGLAM = mc.t[:, M_OFF:M_OFF + 1]
        GDA8 = mc.t[:, M_OFF + 1:M_OFF + 2]

        def cast_w(src, dst, tok, rows, piece=256):
            for r0 in range(0, rows, piece):
                r1 = min(rows, r0 + piece)
                sc.dma("pool", dst.ap()[r0:r1, :], src.ap()[r0:r1, :], writes=[tok], semtok=tok)

        cast_w(f1w12_d, f1w12b, tk["f1w12b"], D)
        cast_w(f1w3_d, f1w3b, tk["f1w3b"], DFF, 704)
        cast_w(win_d, winb, tk["winb"], D)

        with ExitStack() as s0:
            csb = sb(s0, "csb", [128, 8], BF16)
            sc.op("act", lambda e: e.activation(out=csb.t[:], in_=cols.t[:, O_C:O_C + 8], func=AF.Silu),
                  reads=[cols], writes=[csb])
            wa = Pool([sb(s0, "wa%d" % i, [128, 8, 1152], BF16) for i in range(2)])
            modps = ps(s0, "modps", [128, 72])
            for pc in range(8):
                w = wa.next()
                sc.dma("pool", w.t[:], wada_d.ap()[:, pc * 1152:(pc + 1) * 1152].rearrange("(k p) c -> p k c", p=128),
                       writes=[w], semtok=w)
                for qq in range(9):
                    q = pc * 9 + qq
                    for k in range(8):
                        sc.op("pe", lambda e, w=w, qq=qq, k=k, q=q: e.matmul(
                            modps.t[:, q:q + 1], w.t[:, k, qq * 128:(qq + 1) * 128], csb.t[:, k:k + 1],
                            start=(k == 0), stop=(k == 7)), reads=[w, csb], writes=[modps])
            sc.op("dve", lambda e: e.tensor_tensor(mc.t[:, 0:72], modps.t[:, 0:72], cols.t[:, O_BADA:O_BADA + 72], ALU.add),
                  reads=[modps, cols], writes=[mc])
            for s in range(3):
                sc.op("dve", lambda e, s=s: e.scalar_tensor_tensor(
                    mc.t[:, A_OFF + s * 8:A_OFF + s * 8 + 8], mc.t[:, (s * 3 + 1) * 8:(s * 3 + 1) * 8 + 8], 1.0,
                    cols.t[:, O_GN + s * 8:O_GN + s * 8 + 8], ALU.add, ALU.mult), reads=[mc, cols], writes=[mc])
                coef = 1.0 if s == 1 else 0.5
                sc.op("dve", lambda e, s=s, coef=coef: e.tensor_scalar(
                    mc.t[:, G_OFF + s * 8:G_OFF + s * 8 + 8], mc.t[:, (s * 3 + 2) * 8:(s * 3 + 2) * 8 + 8], 1.0, coef,
                    ALU.add, ALU.mult), reads=[mc], writes=[mc])
            lq = sb(s0, "lq", [128, 256], F32)
            sc.dma("sp", lq.t[:], bass.AP(lamq_d, 0, [[0, 128], [1, 256]]), writes=[lq], semtok=lq)
            junk = sb(s0, "junk0", [128, 64], F32)
            l2 = sb(s0, "l2", [128, 4], F32)
            for i in range(2):
                sc.op("dve", lambda e, i=i: e.scalar_tensor_tensor(
                    junk.t[:], lq.t[:, i * 128:i * 128 + 64], 1.0, lq.t[:, i * 128 + 64:i * 128 + 128],
                    ALU.mult, ALU.mult, accum_out=l2.t[:, i:i + 1]), reads=[lq], writes=[junk, l2])
            sc.op("act", lambda e: e.activation(out=l2.t[:, 2:4], in_=l2.t[:, 0:2], func=AF.Exp), reads=[l2], writes=[l2])
            sc.op("dve", lambda e: e.tensor_tensor(l2.t[:, 0:1], l2.t[:, 3:4], l2.t[:, 2:3], ALU.subtract),
                  reads=[l2], writes=[l2])
            sc.op("dve", lambda e: e.tensor_scalar(NEGLAM, l2.t[:, 0:1], -LAMBDA_INIT, None, ALU.add),
                  reads=[l2], writes=[mc])
            sc.op("dve", lambda e: e.tensor_scalar(GDA8, colap(O_GDA), 1.0 - LAMBDA_INIT, None, ALU.mult),
                  reads=[cols], writes=[mc])

        sc.barrier()
        cast_w(wout_d, woutb, tk["woutb"], D)
        cast_w(f2w12_d, f2w12b, tk["f2w12b"], D)
        cast_w(f2w3_d, f2w3b, tk["f2w3b"], DFF, 704)

        def make_ffn_ctx(st, tag, nw12):
            c = {}
            c["X4"] = Pool([sb(st, "X4%s%d" % (tag, i), [128, 4, D], F32) for i in range(2)])
            c["xn"] = Pool([sb(st, "xn%s%d" % (tag, i), [128, D], BF16) for i in range(4)])
            c["sq"] = sb(st, "sqj" + tag, [128, D], BF16)
            c["st"] = Pool([sb(st, "stat%s%d" % (tag, i), [128, 12], F32) for i in range(2)])
            c["hT"] = Pool([sb(st, "hT%s%d" % (tag, i), [128, 8, TT], BF16) for i in range(2)])
            c["gT"] = sb(st, "gT" + tag, [128, NJ, TT], BF16)
            c["w12"] = Pool([sb(st, "w12%s%d" % (tag, i), [128, 2, 8, 256], BF16) for i in range(nw12)])
            c["w3"] = sb(st, "w3" + tag, [128, NJ, D], BF16)
            c["sa"] = Pool([sb(st, "sa%s%d" % (tag, i), [128, TT], F32) for i in range(2)])
            c["oT"] = Pool([sb(st, "oT%s%d" % (tag, i), [128, TT], F32) for i in range(2)])
            c["psA"] = Pool([ps(st, "psA%s%d" % (tag, i), [128, TT]) for i in range(2)])
            c["psB"] = Pool([ps(st, "psB%s%d" % (tag, i), [128, TT]) for i in range(2)])
            c["psO"] = Pool([ps(st, "psO%s%d" % (tag, i), [128, TT]) for i in range(2)])
            c["tp"] = ps(st, "tp" + tag, [128, 8, 128], BF16)
            c["tpo"] = ps(st, "tpo" + tag, [128, 4, 128], F32)
            return c

        def norm_stats(c, X4):
            stt = c["st"].next()
            for st_ in range(4):
                xs = X4.t[:, st_, :]
                sc.op("dve", lambda e, xs=xs, stt=stt, st_=st_: e.scalar_tensor_tensor(
                    c["sq"].t[:], xs, 1.0, xs, ALU.mult, ALU.mult, accum_out=stt.t[:, st_:st_ + 1]),
                    reads=[X4], writes=[c["sq"], stt])
            sc.op("dve", lambda e, stt=stt: e.tensor_scalar(stt.t[:, 4:8], stt.t[:, 0:4], 1.0 / D, EPS, ALU.mult, ALU.add),
                  reads=[stt], writes=[stt])
            sc.op("pool", lambda e, stt=stt: e.tensor_tensor(stt.t[:, 8:12], stt.t[:, 4:8], neghalf.t[:, 0:4], ALU.pow),
                  reads=[stt, neghalf], writes=[stt])
            return stt

        def norm_apply(c, X4, stt, s):
            hT = c["hT"].next()
            xns = []
            for st_ in range(4):
                xn = c["xn"].next()
                sc.op("act", lambda e, xn=xn, st_=st_: e.activation(out=xn.t[:], in_=X4.t[:, st_, :], func=AF.Copy, scale=stt.t[:, 8 + st_:9 + st_]),
                      reads=[X4, stt], writes=[xn])
                xns.append(xn)
            for st_ in range(4):
                xn = xns[st_]
                for k in range(8):
                    sc.op("pe", lambda e, xn=xn, k=k: e.transpose(c["tp"].t[:, k, :], xn.t[:, k * 128:(k + 1) * 128], identb.t[:]),
                          reads=[xn, identb], writes=[c["tp"]])
                for k in range(8):
                    dst = hT.t[:, k, st_ * 128:(st_ + 1) * 128]
                    src = c["tp"].t[:, k, :]
                    if k % 3 == 0:
                        sc.op("act", lambda e, dst=dst, src=src, k=k: e.activation(
                            out=dst, in_=src, func=AF.Identity, bias=Bcol(s, k), scale=Acol(s, k)),
                            reads=[c["tp"], mc], writes=[hT])
                    else:
                        sc.op("dve", lambda e, dst=dst, src=src, k=k: e.tensor_scalar(
                            dst, src, Acol(s, k), Bcol(s, k), ALU.mult, ALU.add), reads=[c["tp"], mc], writes=[hT])
            return hT

        def phaseB(c, src, W, nj, s, X4):
            prev = None

            def finish(oT, dc):
                for st_ in range(4):
                    sc.op("pe", lambda e, oT=oT, st_=st_: e.transpose(
                        c["tpo"].t[:, st_, :], oT.t[:, st_ * 128:(st_ + 1) * 128], identf.t[:]),
                        reads=[oT, identf], writes=[c["tpo"]])
                xv = X4.t[:, :, dc * 128:(dc + 1) * 128]
                sc.op("dve", lambda e, xv=xv: e.tensor_tensor(xv, xv, c["tpo"].t[:], ALU.add),
                      reads=[X4, c["tpo"]], writes=[X4])

            for dc in range(8):
                po = c["psO"].next()
                for j in range(nj):
                    sc.op("pe", lambda e, po=po, j=j, dc=dc: e.matmul(
                        po.t[:], W.t[:, j, dc * 128:(dc + 1) * 128], src.t[:, j, :], start=(j == 0), stop=(j == nj - 1)),
                        reads=[W, src], writes=[po])
                oT = c["oT"].next()
                sc.op("act", lambda e, po=po, oT=oT, dc=dc: e.activation(
                    out=oT.t[:], in_=po.t[:], func=AF.Identity, scale=Gcol(s, dc)), reads=[po, mc], writes=[oT])
                if prev is not None:
                    finish(*prev)
                prev = (oT, dc)
            finish(*prev)

        def ffn(c, hT, w12b, w12tok, s, X4):
            for grp in range(NJ // 2):
                wt = c["w12"].next()
                for ab in range(2):
                    c0 = ab * DFF + grp * 256
                    sc.dma("sp", wt.t[:, ab, :, :], w12b.ap()[:, c0:c0 + 256].rearrange("(k p) c -> p k c", p=128),
                           reads=[w12tok], writes=[wt], semtok=wt)
                for jj in range(2):
                    j = grp * 2 + jj
                    pa = c["psA"].next()
                    pb = c["psB"].next()
                    for ab, pp in ((0, pa), (1, pb)):
                        for k in range(8):
                            sc.op("pe", lambda e, pp=pp, ab=ab, k=k, jj=jj, wt=wt: e.matmul(
                                pp.t[:], wt.t[:, ab, k, jj * 128:(jj + 1) * 128], hT.t[:, k, :], start=(k == 0), stop=(k == 7)),
                                reads=[wt, hT], writes=[pp])
                    sa = c["sa"].next()
                    sc.op("act", lambda e, sa=sa, pa=pa: e.activation(out=sa.t[:], in_=pa.t[:], func=AF.Silu),
                          reads=[pa], writes=[sa])
                    sc.op("dve", lambda e, sa=sa, pb=pb, j=j: e.tensor_tensor(c["gT"].t[:, j, :], sa.t[:], pb.t[:], ALU.mult),
                          reads=[sa, pb], writes=[c["gT"]])
            phaseB(c, c["gT"], c["w3"], NJ, s, X4)

        def load_w3(c, w3b, tok):
            for j0 in range(0, NJ, 11):
                sc.dma("sp", c["w3"].t[:, j0:j0 + 11, :],
                       w3b.ap()[j0 * 128:(j0 + 11) * 128, :].rearrange("(j p) c -> p j c", p=128),
                       reads=[tok], writes=[c["w3"]], semtok=c["w3"])

        with ExitStack() as s1:
            c = make_ffn_ctx(s1, "a", 3)
            load_w3(c, f1w3b, tk["f1w3b"])
            wblk = Pool([sb(s1, "wblk%d" % i, [128, 8, 512], BF16) for i in range(2)])
            wg = sb(s1, "wg", [128, 8, 8], BF16)
            sc.dma("sp", wg.t[:], winb.ap()[:, 3584:3592].rearrange("(k p) c -> p k c", p=128), reads=[tk["winb"]],
                   writes=[wg], semtok=wg)
            cs_t = Pool([sb(s1, "cs%d" % i, [128, 2, TT], F32) for i in range(1)])
            qf = Pool([sb(s1, "qf%d" % i, [128, TT], F32) for i in range(2)])
            sqb = Pool([sb(s1, "sqb%d" % i, [128, TT], BF16) for i in range(2)])
            rs = Pool([sb(s1, "rs%d" % i, [128, TT], F32) for i in range(2)])
            qn = Pool([sb(s1, "qn%d" % i, [128, TT], F32) for i in range(2)])
            qnb = Pool([sb(s1, "qnb%d" % i, [128, TT], BF16) for i in range(2)])
            t1 = Pool([sb(s1, "t1%d" % i, [128, TT], F32) for i in range(2)])
            qr = Pool([sb(s1, "qr%d" % i, [128, TT], BF16) for i in range(2)])
            tmo = Pool([sb(s1, "tmo%d" % i, [128, 4, 512], BF16) for i in range(1)])
            gto = Pool([sb(s1, "gto%d" % i, [128, 4, 8], F32) for i in range(2)])
            zpad = sb(s1, "zpad", [128, 8, 4], BF16)
            sc.op("pool", lambda e: e.memset(zpad.t[:], 0.0), writes=[zpad])
            sc.dma("pool", mlqk_d.ap()[:, :, 0:4].rearrange("g p c -> p g c"), zpad.t[:], reads=[zpad], writes=[tk["mlqk"]],
                   semtok=zpad)
            pw = c["psA"]
            pw2 = c["psB"]
            pending = []

            def tick(keep=0):
                while len(pending) > keep:
                    pending.pop(0)()

            def load_x(ti):
                X4 = c["X4"].next()
                sc.dma("sp", X4.t[:], x_d.ap()[ti * TT:(ti + 1) * TT, :].rearrange("(s p) d -> p s d", p=128),
                       writes=[X4], semtok=X4)
                return X4

            def qk_chunk_B(f_, sq_, blk):
                p2 = pw2.next()
                sc.op("pe", lambda e: e.matmul(p2.t[:], g64.t[:], sq_.t[:], start=True, stop=True),
                      reads=[g64, sq_], writes=[p2])
                r_ = rs.next()
                sc.op("act", lambda e: e.activation(out=r_.t[:], in_=p2.t[:], func=AF.Sqrt, scale=1.0 / 64, bias=epsc.t[:]),
                      reads=[p2, epsc], writes=[r_])
                sc.op("dve", lambda e: e.reciprocal(r_.t[:], r_.t[:]), reads=[r_], writes=[r_])
                gcol = colap(O_GQ if blk == 0 else O_GK)
                n_ = qn.next()
                sc.op("dve", lambda e: e.scalar_tensor_tensor(n_.t[:], f_.t[:], gcol, r_.t[:], ALU.mult, ALU.mult),
                      reads=[f_, r_, cols], writes=[n_])
                nb_ = qnb.next()
                sc.op("act", lambda e: e.activation(out=nb_.t[:], in_=n_.t[:], func=AF.Identity), reads=[n_], writes=[nb_])
                return n_, nb_

            def qk_chunk_C(n_, nb_, cst, g, ti):
                p3 = pw2.next()
                sc.op("pe", lambda e: e.matmul(p3.t[:], ropep.t[:], nb_.t[:], start=True, stop=True),
                      reads=[ropep, nb_], writes=[p3])
                t_ = t1.next()
                sc.op("dve", lambda e: e.tensor_tensor(t_.t[:], n_.t[:], cst.t[:, 0, :], ALU.mult), reads=[n_, cst], writes=[t_])
                sc.op("dve", lambda e: e.tensor_tensor(n_.t[:], p3.t[:], cst.t[:, 1, :], ALU.mult), reads=[p3, cst], writes=[n_])
                o = qr.next()
                sc.op("dve", lambda e: e.tensor_tensor(o.t[:], t_.t[:], n_.t[:], ALU.add), reads=[t_, n_], writes=[o])
                sc.dma("pool", daT_d.ap()[g, :, ti * TT:(ti + 1) * TT], o.t[:], reads=[o], writes=[tk["daT"]], semtok=o)

            X4 = load_x(0)
            st0 = norm_stats(c, X4)
            hT = norm_apply(c, X4, st0, 0)
            for ti in range(NT):
                Xn = stn = None
                if ti + 1 < NT:
                    Xn = load_x(ti + 1)
                    stn = norm_stats(c, Xn)
                ffn(c, hT, f1w12b, tk["f1w12b"], 0, X4)
                sc.dma("pool", x1_d.ap()[ti * TT:(ti + 1) * TT, :].rearrange("(s p) d -> p s d", p=128), X4.t[:],
                       reads=[X4], writes=[tk["x1"]], semtok=X4)
                st1 = norm_stats(c, X4)
                if Xn is not None:
                    hT = norm_apply(c, Xn, stn, 0)
                h2 = norm_apply(c, X4, st1, 1)
                cst = cs_t.next()
                sc.dma("sp", cst.t[:, 0, :], cos_d.ap()[:, ti * TT:(ti + 1) * TT], writes=[cst], semtok=cst)
                sc.dma("sp", cst.t[:, 1, :], sin_d.ap()[:, ti * TT:(ti + 1) * TT], writes=[cst], semtok=cst)
                for blk in (0, 1, 3, 4, 2, 5, 6):
                    wb_ = wblk.next()
                    sc.dma("sp", wb_.t[:], winb.ap()[:, blk * 512:(blk + 1) * 512].rearrange("(k p) c -> p k c", p=128),
                           reads=[tk["winb"]], writes=[wb_], semtok=wb_)
                    if blk in (0, 1, 3, 4):
                        for cc in range(4):
                            pp = pw.next()
                            for k in range(8):
                                sc.op("pe", lambda e, pp=pp, k=k, cc=cc, wb_=wb_: e.matmul(
                                    pp.t[:], wb_.t[:, k, cc * 128:(cc + 1) * 128], h2.t[:, k, :], start=(k == 0), stop=(k == 7)),
                                    reads=[wb_, h2], writes=[pp])
                            if blk in (3, 4):
                                o = qr.next()
                                sc.op("act", lambda e, o=o, pp=pp: e.activation(out=o.t[:], in_=pp.t[:], func=AF.Identity),
                                      reads=[pp], writes=[o])
                                g = (blk - 3) * 4 + cc
                                sc.dma("pool", mlqk_d.ap()[g, :, 4 + ti * TT:4 + (ti + 1) * TT], o.t[:], reads=[o],
                                       writes=[tk["mlqk"]], semtok=o)
                                tick(keep=max(0, len(pending) - 1))
                                continue
                            f_ = qf.next()
                            sc.op("act", lambda e, f_=f_, pp=pp: e.activation(out=f_.t[:], in_=pp.t[:], func=AF.Identity),
                                  reads=[pp], writes=[f_])
                            sq_ = sqb.next()
                            sc.op("act", lambda e, sq_=sq_, pp=pp: e.activation(out=sq_.t[:], in_=pp.t[:], func=AF.Square),
                                  reads=[pp], writes=[sq_])
                            g = blk * 4 + cc
                            box = {}

                            def stepB(f_=f_, sq_=sq_, blk=blk, box=box):
                                box["v"] = qk_chunk_B(f_, sq_, blk)

                            def stepC(box=box, cst=cst, g=g, ti=ti):
                                qk_chunk_C(box["v"][0], box["v"][1], cst, g, ti)

                            tick(keep=1)
                            pending.append(stepB)
                            pending.append(stepC)
                    else:
                        o = tmo.next()
                        for st_ in range(4):
                            pp = pw.next()
                            for k in range(8):
                                sc.op("pe", lambda e, pp=pp, k=k, st_=st_, wb_=wb_: e.matmul(
                                    pp.t[:], h2.t[:, k, st_ * 128:(st_ + 1) * 128], wb_.t[:, k, :], start=(k == 0), stop=(k == 7)),
                                    reads=[wb_, h2], writes=[pp])
                            if st_ % 2 == 0:
                                sc.op("act", lambda e, o=o, pp=pp, st_=st_: e.activation(out=o.t[:, st_, :], in_=pp.t[:], func=AF.Identity),
                                      reads=[pp], writes=[o])
                            else:
                                sc.op("dve", lambda e, o=o, pp=pp, st_=st_: e.tensor_copy(o.t[:, st_, :], pp.t[:]),
                                      reads=[pp], writes=[o])
                            tick(keep=max(0, len(pending) - 1))
                        dd, tkn = {2: (dav_d, "dav"), 5: (mlv_d, "mlv"), 6: (mlo_d, "mlo")}[blk]
                        sc.dma("pool", dd.ap()[ti * TT:(ti + 1) * TT, :].rearrange("(s p) c -> p s c", p=128), o.t[:],
                               reads=[o], writes=[tk[tkn]], semtok=o)
                tick()
                go = gto.next()
                pp = pw2.next()
                for st_ in range(4):
                    for k in range(8):
                        sc.op("pe", lambda e, pp=pp, k=k, st_=st_: e.matmul(
                            pp.t[:, st_ * 8:(st_ + 1) * 8], h2.t[:, k, st_ * 128:(st_ + 1) * 128], wg.t[:, k, :],
                            start=(k == 0), stop=(k == 7)), reads=[wg, h2], writes=[pp])
                sc.op("dve", lambda e, go=go, pp=pp: e.tensor_copy(go.t[:].rearrange("p s c -> p (s c)"), pp.t[:, 0:32]),
                      reads=[pp], writes=[go])
                sc.dma("pool", gat_d.ap()[ti * TT:(ti + 1) * TT, :].rearrange("(s p) c -> p s c", p=128), go.t[:], reads=[go],
                       writes=[tk["gat"]], semtok=go)
                X4 = Xn

        sc.barrier()
        with ExitStack() as s2:
            QT = Pool([sb(s2, "QT%d" % i, [128, S], BF16) for i in range(2)])
            KT = Pool([[sb(s2, "KT%d_%d" % (i, j), [128, S], BF16) for j in range(2)] for i in range(2)])
            for kp in KT.items:
                sc.op("pool", lambda e, kp=kp: e.memset(kp[0].t[64:128, :], 0.0), writes=[kp[0]])
                sc.op("pool", lambda e, kp=kp: e.memset(kp[1].t[0:64, :], 0.0), writes=[kp[1]])
            VV = Pool([sb(s2, "VV%d" % i, [128, 32, 128], BF16) for i in range(2)])
            yTh = Pool([sb(s2, "yTh%d" % i, [128, S], BF16) for i in range(2)])
            Et = Pool([sb(s2, "Et%d" % i, [128, TT], BF16) for i in range(4)])
            wk = Pool([ps(s2, "wk%d" % i, [128, TT]) for i in range(4)])
            oacc = [ps(s2, "oacc%d" % i, [128, TT]) for i in range(2)]
            dacc = [ps(s2, "dacc%d" % i, [128, TT]) for i in range(2)]
            osb = [sb(s2, "osb%d" % i, [128, TT], F32) for i in range(2)]
            rdn = [sb(s2, "rdn%d" % i, [128, TT], F32) for i in range(2)]
            ddt = sb(s2, "ddt", [128, TT], F32)
            sq2 = sb(s2, "sq2", [128, TT], BF16)
            rs2 = sb(s2, "rs2", [128, TT], F32)
            LAG = 2
            for h in range(4):
                q_, k_, v_, y_ = QT.next(), KT.next(), VV.next(), yTh.next()
                sc.dma("sp", q_.t[:], daT_d.ap()[h], reads=[tk["daT"]], writes=[q_], semtok=q_)
                sc.dma("sp", k_[0].t[0:64, :], daT_d.ap()[4 + h, 0:64, :], reads=[tk["daT"]], writes=[k_[0]], semtok=k_[0])
                sc.dma("sp", k_[1].t[64:128, :], daT_d.ap()[4 + h, 64:128, :], reads=[tk["daT"]], writes=[k_[1]], semtok=k_[1])
                sc.dma("sp", v_.t[:], dav_d.ap()[:, h * 128:(h + 1) * 128].rearrange("(t p) c -> p t c", p=128),
                       reads=[tk["dav"]], writes=[v_], semtok=v_)
                steps = []
                for qb in range(8):
                    nkb = 4 * (qb + 1)
                    for kb in range(nkb):
                        for cpt in range(2):
                            steps.append((qb, kb, cpt, nkb))

                def epilogue(qb, y_=y_):
                    for cpt in range(2):
                        sc.op("act", lambda e, cpt=cpt: e.activation(out=osb[cpt].t[:], in_=oacc[cpt].t[:], func=AF.Identity),
                              reads=[oacc[cpt]], writes=[osb[cpt]])
                        sc.op("dve", lambda e, cpt=cpt: e.reciprocal(rdn[cpt].t[:], dacc[cpt].t[:]),
                              reads=[dacc[cpt]], writes=[rdn[cpt]])
                        sc.op("pool", lambda e, cpt=cpt: e.tensor_tensor(osb[cpt].t[:], osb[cpt].t[:], rdn[cpt].t[:], ALU.mult),
                              reads=[osb[cpt], rdn[cpt]], writes=[osb[cpt]])
                    sc.op("dve", lambda e: e.scalar_tensor_tensor(ddt.t[:], osb[1].t[:], NEGLAM, osb[0].t[:], ALU.mult, ALU.add),
                          reads=[osb[0], osb[1], mc], writes=[ddt])
                    sc.op("pool", lambda e: e.tensor_tensor(sq2.t[:], ddt.t[:], ddt.t[:], ALU.mult), reads=[ddt], writes=[sq2])
                    p2 = wk.next()
                    sc.op("pe", lambda e, p2=p2: e.matmul(p2.t[:], onesb.t[:], sq2.t[:], start=True, stop=True),
                          reads=[onesb, sq2], writes=[p2])
                    sc.op("act", lambda e, p2=p2: e.activation(out=rs2.t[:], in_=p2.t[:], func=AF.Sqrt, scale=1.0 / 128, bias=epsc.t[:]),
                          reads=[p2, epsc], writes=[rs2])
                    sc.op("dve", lambda e: e.reciprocal(rs2.t[:], rs2.t[:]), reads=[rs2], writes=[rs2])
                    sc.op("dve", lambda e, qb=qb: e.scalar_tensor_tensor(
                        y_.t[:, qb * TT:(qb + 1) * TT], ddt.t[:], GDA8, rs2.t[:], ALU.mult, ALU.mult),
                        reads=[ddt, rs2, mc], writes=[y_])

                inflight = []
                n = len(steps)
                for i in range(n + LAG):
                    if i < n:
                        qb, kb, cpt, nkb = steps[i]
                        r = kb - 4 * qb
                        q0 = 128 * r if r > 0 else 0
                        stp = wk.next()
                        sc.op("pe", lambda e, stp=stp, cpt=cpt, kb=kb, q0=q0, qb=qb: e.matmul(
                            stp.t[:, q0:TT], k_[cpt].t[:, kb * 128:(kb + 1) * 128], q_.t[:, qb * TT + q0:(qb + 1) * TT],
                            start=True, stop=True), reads=[k_[cpt], q_], writes=[stp])
                        E = Et.next()
                        sc.op("act", lambda e, E=E, stp=stp, q0=q0: e.activation(
                            out=E.t[:, q0:TT], in_=stp.t[:, q0:TT], func=AF.Exp, scale=0.125), reads=[stp], writes=[E])
                        if r >= 0:
                            sc.op("dve", lambda e, E=E, q0=q0: e.tensor_tensor(
                                E.t[:, q0:q0 + 128], E.t[:, q0:q0 + 128], trib.t[:], ALU.mult), reads=[E, trib], writes=[E])
                        inflight.append((E, q0))
                    if i >= LAG:
                        qb, kb, cpt, nkb = steps[i - LAG]
                        E, q0 = inflight.pop(0)
                        sc.op("pe", lambda e, E=E, cpt=cpt, kb=kb, q0=q0, nkb=nkb: e.matmul(
                            oacc[cpt].t[:, q0:TT], v_.t[:, kb, :], E.t[:, q0:TT], start=(kb == 0), stop=(kb == nkb - 1),
                            skip_group_check=True), reads=[v_, E], writes=[oacc[cpt]])
                        sc.op("pe", lambda e, E=E, cpt=cpt, kb=kb, q0=q0, nkb=nkb: e.matmul(
                            dacc[cpt].t[:, q0:TT], onesb.t[:], E.t[:, q0:TT], start=(kb == 0), stop=(kb == nkb - 1),
                            skip_group_check=True), reads=[onesb, E], writes=[dacc[cpt]])
                        if kb == nkb - 1 and cpt == 1:
                            epilogue(qb)
                sc.dma("pool", yT_d.ap()[h], y_.t[:], reads=[y_], writes=[tk["yT"]], semtok=y_)

        sc.barrier()
        with ExitStack() as s3:
            LNS = math.log(128.0 ** -0.5)
            gt = sb(s3, "gt", [128, 32, 8], F32)
            sc.dma("sp", gt.t[:], gat_d.ap().rearrange("(c p) g -> p c g", p=128), reads=[tk["gat"]], writes=[gt], semtok=gt)
            bif = sb(s3, "bif", [128, 8], F32)
            sc.dma("sp", bif.t[:], bass.AP(bif_d, 0, [[0, 128], [1, 8]]), writes=[bif], semtok=bif)
            lnsc = sb(s3, "lnsc", [128, 1], F32)
            sc.op("pool", lambda e: e.memset(lnsc.t[:], LNS), writes=[lnsc])
            raw = Pool([sb(s3, "raw%d" % i, [128, S + 4], BF16) for i in range(2)])
            qTs = Pool([sb(s3, "mqT%d" % i, [128, S], BF16) for i in range(2)])
            kTs = Pool([sb(s3, "mkT%d" % i, [128, S], BF16) for i in range(2)])
            kSs = Pool([sb(s3, "mkS%d" % i, [128, 32, 128], BF16) for i in range(2)])
            vas = Pool([sb(s3, "mva%d" % i, [128, 32, 129], BF16) for i in range(2)])
            for v in vas.items:
                sc.op("pool", lambda e, v=v: e.memset(v.t[:, :, 128:129], 1.0), writes=[v])
            ogs = Pool([sb(s3, "mog%d" % i, [128, 32, 128], BF16) for i in range(2)])
            yhs = Pool([sb(s3, "myh%d" % i, [128, S], BF16) for i in range(2)])
            gss = Pool([sb(s3, "gs%d" % i, [128, 32 * 9], F32) for i in range(2)])
            cacc = Pool([sb(s3, "cacc%d" % i, [128, 1024], F32) for i in range(2)])
            C32 = sb(s3, "C32", [128, 129], F32)
            Cb = Pool([sb(s3, "Cb%d" % i, [128, 129], BF16) for i in range(2)])
            WT = Pool([sb(s3, "WT%d" % i, [128, 128], BF16) for i in range(3)])
            ytm = Pool([sb(s3, "ytm%d" % i, [128, 128], BF16) for i in range(3)])
            sm = Pool([sb(s3, "sm%d" % i, [128, 8], F32) for i in range(4)])
            jk = sb(s3, "jk", [128, 128], BF16)
            pST = Pool([ps(s3, "pST%d" % i, [128, 128]) for i in range(2)])
            pZ = Pool([ps(s3, "pZ%d" % i, [128, 129]) for i in range(3)])
            pU = Pool([ps(s3, "pU%d" % i, [128, 129]) for i in range(1)])
            pT = ps(s3, "pT", [128, 8, 128], BF16)
            pG = ps(s3, "pG", [128, 64])

            def prep_head(h):
                gs, qT, kT, kS, va, og, yh = gss.next(), qTs.next(), kTs.next(), kSs.next(), vas.next(), ogs.next(), yhs.next()

                def G(i, c0=0, c1=32):
                    return gs.t[:, i * 32 + c0:i * 32 + c1]

                sc.op("dve", lambda e: e.tensor_scalar(G(0), gt.t[:, :, h], bif.t[:, h:h + 1], None, ALU.add),
                      reads=[gt, bif], writes=[gs])
                sc.op("dve", lambda e: e.tensor_scalar(G(1), gt.t[:, :, 4 + h], bif.t[:, 4 + h:5 + h], None, ALU.add),
                      reads=[gt, bif], writes=[gs])
                sc.op("act", lambda e: e.activation(out=G(2), in_=G(1), func=AF.Exp, scale=-1.0), reads=[gs], writes=[gs])
                sc.op("act", lambda e: e.activation(out=G(2), in_=G(2), func=AF.Ln, bias=1.0), reads=[gs], writes=[gs])
                sc.op("pe", lambda e: e.matmul(pG.t[:, 0:32], trif.t[:], G(2), start=True, stop=True),
                      reads=[trif, gs], writes=[pG])
                sc.op("pe", lambda e: e.matmul(pG.t[:, 32:64], onesf.t[:], G(2), start=True, stop=True),
                      reads=[onesf, gs], writes=[pG])
                sc.op("dve", lambda e: e.tensor_tensor(G(3), pG.t[:, 0:32], G(0), ALU.add), reads=[pG, gs], writes=[gs])
                sc.op("act", lambda e: e.activation(out=G(4), in_=G(3), func=AF.Exp, bias=lnsc.t[:]), reads=[gs, lnsc], writes=[gs])
                sc.op("act", lambda e: e.activation(out=G(5), in_=pG.t[:, 0:32], func=AF.Exp), reads=[pG], writes=[gs])
                sc.op("dve", lambda e: e.tensor_tensor(G(6), G(3), pG.t[:, 32:64], ALU.subtract), reads=[pG, gs], writes=[gs])
                sc.op("act", lambda e: e.activation(out=G(7), in_=G(6), func=AF.Exp, bias=lnsc.t[:]), reads=[gs, lnsc], writes=[gs])
                sc.op("act", lambda e: e.activation(out=G(8), in_=pG.t[:, 32:64], func=AF.Exp, scale=-1.0), reads=[pG], writes=[gs])
                for which, dst in ((0, qT), (1, kT)):
                    rw = raw.next()
                    ch = which * 4 + h
                    sc.dma("sp", rw.t[:], mlqk_d.ap()[ch], reads=[tk["mlqk"]], writes=[rw], semtok=rw)
                    for pc in range(4):
                        o0 = pc * 1024
                        ca = cacc.next()
                        sc.op("dve", lambda e, ca=ca, rw=rw, o0=o0, ch=ch: e.tensor_scalar(
                            ca.t[:], rw.t[:, 1 + o0:1 + o0 + 1024], colap(O_CW + ch), colap(O_CB + ch), ALU.mult, ALU.add),
                            reads=[rw, cols], writes=[ca])
                        for j in range(1, 4):
                            sc.op("dve", lambda e, ca=ca, rw=rw, o0=o0, ch=ch, j=j: e.scalar_tensor_tensor(
                                ca.t[:], rw.t[:, 1 + o0 + j:1 + o0 + j + 1024], colap(O_CW + j * 8 + ch), ca.t[:], ALU.mult, ALU.add),
                                reads=[rw, cols, ca], writes=[ca])
                        sc.op("act", lambda e, ca=ca, dst=dst, o0=o0: e.activation(out=dst.t[:, o0:o0 + 1024], in_=ca.t[:], func=AF.Silu),
                              reads=[ca], writes=[dst])
                for c8 in range(4):
                    for i in range(8):
                        cch = c8 * 8 + i
                        sc.op("pe", lambda e, i=i, cch=cch: e.transpose(pT.t[:, i, :], kT.t[:, cch * 128:(cch + 1) * 128], identb.t[:]),
                              reads=[kT, identb], writes=[pT])
                    for i in range(8):
                        cch = c8 * 8 + i
                        if i % 2 == 0:
                            sc.op("dve", lambda e, i=i, cch=cch: e.tensor_scalar(kS.t[:, cch, :], pT.t[:, i, :], G(7, cch, cch + 1), None, ALU.mult),
                                  reads=[pT, gs], writes=[kS])
                        else:
                            sc.op("act", lambda e, i=i, cch=cch: e.activation(out=kS.t[:, cch, :], in_=pT.t[:, i, :], func=AF.Copy, scale=G(7, cch, cch + 1)),
                                  reads=[pT, gs], writes=[kS])
                sc.dma("sp", va.t[:, :, 0:128], mlv_d.ap()[:, h * 128:(h + 1) * 128].rearrange("(c p) d -> p c d", p=128),
                       reads=[tk["mlv"]], writes=[va], semtok=va)
                sc.dma("sp", og.t[:], mlo_d.ap()[:, h * 128:(h + 1) * 128].rearrange("(c p) d -> p c d", p=128),
                       reads=[tk["mlo"]], writes=[og], semtok=og)
                sc.op("act", lambda e: e.activation(out=og.t[:], in_=og.t[:], func=AF.Sigmoid), reads=[og], writes=[og])
                return dict(G=G, gs=gs, qT=qT, kT=kT, kS=kS, va=va, og=og, yh=yh, h=h)

            def run_head(H):
                G, gs, qT, kT, kS, va, og, yh, h = (H[k] for k in ("G", "gs", "qT", "kT", "kS", "va", "og", "yh", "h"))
                st = {"cb": None}
                wts, zs, yts = {}, {}, {}

                def sA(cch):
                    cs_ = slice(cch * 128, (cch + 1) * 128)
                    st_p = pST.next()
                    sc.op("pe", lambda e: e.matmul(st_p.t[:, 0:128], kT.t[:, cs_], qT.t[:, cs_], start=True, stop=True),
                          reads=[kT, qT], writes=[st_p])
                    wt_ = WT.next()
                    sc.op("dve", lambda e: e.scalar_tensor_tensor(
                        wt_.t[:], st_p.t[:, 0:128], G(4, cch, cch + 1), trib.t[:], ALU.mult, ALU.mult),
                        reads=[st_p, gs, trib], writes=[wt_])
                    wts[cch] = wt_

                def sB(cch):
                    cs_ = slice(cch * 128, (cch + 1) * 128)
                    wt_ = wts.pop(cch)
                    z = pZ.next()
                    sc.op("pe", lambda e: e.matmul(z.t[:, 0:129], wt_.t[:], va.t[:, cch, :], start=True, stop=(cch == 0)),
                          reads=[wt_, va], writes=[z])
                    if cch > 0:
                        cbp = st["cb"]
                        sc.op("pe", lambda e: e.matmul(z.t[:, 0:129], qT.t[:, cs_], cbp.t[:], start=False, stop=True),
                              reads=[qT, cbp], writes=[z])
                    if cch < 31:
                        u = pU.next()
                        sc.op("pe", lambda e: e.matmul(u.t[:, 0:129], kS.t[:, cch, :], va.t[:, cch, :], start=True, stop=True),
                              reads=[kS, va], writes=[u])
                        if cch == 0:
                            sc.op("dve", lambda e: e.tensor_copy(C32.t[:], u.t[:, 0:129]), reads=[u], writes=[C32])
                        else:
                            sc.op("dve", lambda e: e.scalar_tensor_tensor(
                                C32.t[:], C32.t[:], G(8, cch, cch + 1), u.t[:, 0:129], ALU.mult, ALU.add), reads=[u, C32, gs], writes=[C32])
                        cbn = Cb.next()
                        sc.op("act", lambda e: e.activation(out=cbn.t[:], in_=C32.t[:], func=AF.Copy), reads=[C32], writes=[cbn])
                        st["cb"] = cbn
                    m = sm.next()
                    sc.op("dve", lambda e: e.tensor_scalar(m.t[:, 7:8], z.t[:, 128:129], -1.0, None, ALU.mult), reads=[z], writes=[m])
                    sc.op("dve", lambda e: e.scalar_tensor_tensor(
                        m.t[:, 0:1], z.t[:, 128:129], G(5, cch, cch + 1), m.t[:, 7:8], ALU.max, ALU.max), reads=[z, gs, m], writes=[m])
                    sc.op("dve", lambda e: e.reciprocal(m.t[:, 1:2], m.t[:, 0:1]), reads=[m], writes=[m])
                    sc.op("act", lambda e: e.activation(out=jk.t[:], in_=z.t[:, 0:128], func=AF.Square, accum_out=m.t[:, 2:3]),
                          reads=[z], writes=[jk, m])
                    sc.op("dve", lambda e: e.scalar_tensor_tensor(m.t[:, 3:4], m.t[:, 2:3], m.t[:, 1:2], m.t[:, 1:2], ALU.mult, ALU.mult),
                          reads=[m], writes=[m])
                    sc.op("dve", lambda e: e.tensor_scalar(m.t[:, 4:5], m.t[:, 3:4], 1.0 / 128, EPS, ALU.mult, ALU.add),
                          reads=[m], writes=[m])
                    sc.op("pool", lambda e: e.tensor_tensor(m.t[:, 5:6], m.t[:, 4:5], neghalf.t[:, 0:1], ALU.pow),
                          reads=[m, neghalf], writes=[m])
                    sc.op("pool", lambda e: e.tensor_tensor(m.t[:, 6:7], m.t[:, 5:6], m.t[:, 1:2], ALU.mult), reads=[m], writes=[m])
                    yt = ytm.next()
                    sc.op("dve", lambda e: e.scalar_tensor_tensor(
                        yt.t[:], z.t[:, 0:128], m.t[:, 6:7], og.t[:, cch, :], ALU.mult, ALU.mult), reads=[z, m, og], writes=[yt])
                    yts[cch] = yt

                def sC(cch):
                    cs_ = slice(cch * 128, (cch + 1) * 128)
                    yt = yts.pop(cch)
                    i8 = cch % 8
                    sc.op("pe", lambda e: e.transpose(pT.t[:, i8, :], yt.t[:], identb.t[:]), reads=[yt, identb], writes=[pT])
                    sc.op("act", lambda e: e.activation(out=yh.t[:, cs_], in_=pT.t[:, i8, :], func=AF.Copy, scale=colap(O_GML + h)),
                          reads=[pT, cols], writes=[yh])

                for i in range(32 + 3):
                    if i < 32:
                        sA(i)
                    if 0 <= i - 1 < 32:
                        sB(i - 1)
                    if 0 <= i - 3 < 32:
                        sC(i - 3)
                sc.dma("pool", yT_d.ap()[4 + h], yh.t[:], reads=[yh], writes=[tk["yT"]], semtok=yh)

            Hn = prep_head(0)
            for h in range(4):
                Hc = Hn
                run_head(Hc)
                if h + 1 < 4:
                    Hn = prep_head(h + 1)

        sc.barrier()
        with ExitStack() as s4:
            c = make_ffn_ctx(s4, "b", 4)
            load_w3(c, f2w3b, tk["f2w3b"])
            wo = sb(s4, "wo", [128, 8, D], BF16)
            sc.dma("sp", wo.t[:], woutb.ap().rearrange("(m p) c -> p m c", p=128), reads=[tk["woutb"]], writes=[wo], semtok=wo)
            yTt = Pool([sb(s4, "yTt%d" % i, [128, 8, TT], BF16) for i in range(2)])

            def load4(ti):
                X4 = c["X4"].next()
                sc.dma("sp", X4.t[:], x1_d.ap()[ti * TT:(ti + 1) * TT, :].rearrange("(s p) d -> p s d", p=128),
                       reads=[tk["x1"]], writes=[X4], semtok=X4)
                y = yTt.next()
                sc.dma("sp", y.t[:], yT_d.ap()[:, :, ti * TT:(ti + 1) * TT].rearrange("m p t -> p m t"), reads=[tk["yT"]],
                       writes=[y], semtok=y)
                return X4, y

            nxt = load4(0)
            for ti in range(NT):
                X4, y = nxt
                phaseB(c, y, wo, 8, 1, X4)
                if ti + 1 < NT:
                    nxt = load4(ti + 1)
                st2 = norm_stats(c, X4)
                hT = norm_apply(c, X4, st2, 2)
                ffn(c, hT, f2w12b, tk["f2w12b"], 2, X4)
                sc.dma("pool", out_d.ap()[ti * TT:(ti + 1) * TT, :].rearrange("(s p) d -> p s d", p=128), X4.t[:],
                       reads=[X4], writes=[tk["out"]], semtok=X4)
            sc.wait_all("sp", [tk["out"]])
    return nc


def _consts():
    bf = ml_dtypes.bfloat16
    i = np.arange(128)
    tri = (i[:, None] <= i[None, :]).astype(np.float32)
    g64 = ((i[:, None] // 64) == (i[None, :] // 64)).astype(np.float32)
    P = np.zeros((128, 128), np.float32)
    for d in range(128):
        m = d % 64
        if m < 8:
            P[d + 8, d] = -1.0
        elif m < 16:
            P[d - 8, d] = 1.0
    inv_freq = (500000.0 ** (-(np.arange(0, 16, 2).astype(np.float32)) / np.float32(16))).astype(np.float32)
    pos = np.arange(S, dtype=np.float32)
    ang = (pos[:, None] * inv_freq[None, :]).astype(np.float32)
    cosT = np.ones((128, S), np.float32)
    sinT = np.zeros((128, S), np.float32)
    for d in range(128):
        m = d % 64
        if m < 16:
            cosT[d] = np.cos(ang[:, m % 8].astype(np.float64)).astype(np.float32)
            sinT[d] = np.sin(ang[:, m % 8].astype(np.float64)).astype(np.float32)
    return {
        "identb": np.eye(128, dtype=np.float32).astype(bf), "identf": np.eye(128, dtype=np.float32),
        "trib": tri.astype(bf), "trif": tri, "onesb": np.ones((128, 128), np.float32).astype(bf),
        "onesf": np.ones((128, 128), np.float32), "g64": g64.astype(bf), "ropep": P.astype(bf),
        "cosT": cosT, "sinT": sinT,
    }


def _in_maps(inp):
    f = lambda a: np.ascontiguousarray(np.asarray(a, dtype=np.float32))
    consts = _consts()
    col = lambda v, n: f(v).reshape(n, 128).T
    g_q = f(inp["g_qnorm"])[0]
    g_k = f(inp["g_knorm"])[0]
    shared = {
        "w_ada": f(inp["w_ada"])[0], "ffn1_w12": f(inp["ffn1_w12"])[0], "ffn1_w3": f(inp["ffn1_w3"])[0],
        "w_in": f(inp["w_in"])[0], "w_out": f(inp["w_out"])[0], "ffn2_w12": f(inp["ffn2_w12"])[0],
        "ffn2_w3": f(inp["ffn2_w3"])[0],
        "lamq": f(inp["lambda_qk"])[0].reshape(1, 256),
        "bif": np.concatenate([f(inp["b_igate"])[0], f(inp["b_fgate"])[0]]).reshape(1, 8),
    }
    shared.update(consts)
    cw = f(inp["conv_w"])[0].reshape(4, 8, 128).transpose(2, 0, 1).reshape(128, 32)
    maps = []
    x = f(inp["x"])
    c = f(inp["c"])
    for b in range(8):
        cols = np.concatenate([
            col(inp["b_ada"][0], 72), col(inp["g_norm"][0], 24), col(c[b], 8), cw, col(inp["conv_b"][0], 8),
            f(inp["g_da_out"])[0].reshape(128, 1), np.concatenate([g_q, g_q]).reshape(128, 1),
            np.concatenate([g_k, g_k]).reshape(128, 1), col(inp["g_ml_out"][0], 4)], axis=1)
        m = dict(shared)
        m["x"] = x[b]
        m["cols"] = np.ascontiguousarray(cols.astype(np.float32))
        maps.append(m)
    return maps


def kernel(**inputs):
    nc = build()
    res = run_bass_kernel_spmd(nc, _in_maps(inputs), core_ids=list(range(8)))
    return np.stack([np.asarray(r["out"], dtype=np.float32) for r in res.results], axis=0)
```
